# Optimizing a Trainium2 kernel written in Bass

```python
import jax, jax.numpy as jnp
from jax import lax
import numpy as np

D_MODEL = 1024
BATCH = 2
SEQ = 16384
DEPTH = 2

D_FF = 2816
N_SUB = 3
W_A = D_MODEL // 2
W_B = D_MODEL // 2
H_A = 8
DH_A = W_A // H_A
G_B = 8
DG_B = W_B // G_B
CONV_A = 4
CONV_B = 31
LRU_C = 8.0
W_C = D_MODEL
H_C = 8
DH_C = W_C // H_C
CHUNK = 128
EPS = 1e-6

kernel_name = "hybrid_rglru_conformer_gmlp_block"


def _rmsnorm(x, g):
    x32 = x.astype(jnp.float32)
    y = x32 * lax.rsqrt(jnp.mean(x32 * x32, axis=-1, keepdims=True) + EPS)
    return (y * g.astype(jnp.float32)).astype(x.dtype)


def _layernorm(x, g, b):
    x32 = x.astype(jnp.float32)
    mu = jnp.mean(x32, axis=-1, keepdims=True)
    var = jnp.mean(jnp.square(x32 - mu), axis=-1, keepdims=True)
    y = (x32 - mu) * lax.rsqrt(var + EPS)
    return (y * g.astype(jnp.float32) + b.astype(jnp.float32)).astype(x.dtype)


def _causal_dwconv(x, w, b):
    k = w.shape[0]
    y = lax.conv_general_dilated(
        x, w[:, None, :].astype(x.dtype), window_strides=(1,), padding=[(k - 1, 0)],
        dimension_numbers=("NWC", "WIO", "NWC"), feature_group_count=x.shape[-1])
    return y + b


def _swiglu(h, w13, w2):
    g, u = jnp.split(h @ w13, 2, axis=-1)
    return (jax.nn.silu(g) * u) @ w2


def _rg_lru(xr, gate_w, gate_b, lam):
    bsz, s, _ = xr.shape
    xh = xr.reshape(bsz, s, H_A, DH_A)
    gates = (jnp.einsum("bshd,hde->bshe", xh, gate_w) + gate_b).astype(jnp.float32)
    r, i = jnp.split(jax.nn.sigmoid(gates), 2, axis=-1)
    r = r.reshape(bsz, s, W_A)
    i = i.reshape(bsz, s, W_A)
    log_a = LRU_C * r * jax.nn.log_sigmoid(lam.astype(jnp.float32))
    a = jnp.exp(log_a)
    u = jnp.sqrt(-jnp.expm1(2.0 * log_a)) * (i * xr.astype(jnp.float32))

    def combine(left, right):
        a1, b1 = left
        a2, b2 = right
        return a1 * a2, a2 * b1 + b2

    _, h = lax.associative_scan(combine, (a, u), axis=1)
    return h.astype(xr.dtype)


def _mixer_ab(h, w_in, a_conv_w, a_conv_b, a_gate_w, a_gate_b, a_lam,
              b_conv_w, b_conv_b, b_norm_g, b_norm_b, w_out):
    z = h @ w_in
    a_gate, a_x, b_val, b_gate = jnp.split(z, [W_A, 2 * W_A, 2 * W_A + W_B], axis=-1)
    a_x = _causal_dwconv(a_x, a_conv_w, a_conv_b)
    y_a = _rg_lru(a_x, a_gate_w, a_gate_b, a_lam) * jax.nn.gelu(a_gate)
    v = b_val * jax.nn.sigmoid(b_gate)
    v = _causal_dwconv(v, b_conv_w, b_conv_b)
    bsz, s, _ = v.shape
    v = _layernorm(v.reshape(bsz, s, G_B, DG_B), b_norm_g.reshape(G_B, DG_B),
                   b_norm_b.reshape(G_B, DG_B)).reshape(bsz, s, W_B)
    y_b = jax.nn.silu(v)
    return jnp.concatenate([y_a, y_b], axis=-1) @ w_out


def _mixer_c(h, w_in, b_in, norm_g, norm_b, w_s, b_s, w_out):
    z = jax.nn.gelu(h @ w_in + b_in)
    u, v = jnp.split(z, 2, axis=-1)
    v = _layernorm(v, norm_g, norm_b)
    bsz, s, _ = v.shape
    vc = v.reshape(bsz, s // CHUNK, CHUNK, H_C, DH_C)
    mask = jnp.tril(jnp.ones((CHUNK, CHUNK), dtype=bool))
    ws = jnp.where(mask, w_s, jnp.zeros_like(w_s)).astype(v.dtype)
    mixed = jnp.einsum("hts,bnshd->bnthd", ws, vc) + jnp.transpose(b_s)[:, :, None]
    return (u * mixed.reshape(bsz, s, W_C)) @ w_out


def _sublayer(x, fn, pre_g, post_g, shift, scale, gate, res_w):
    h = _rmsnorm(x, pre_g) * (1.0 + scale[:, None, :]) + shift[:, None, :]
    y = _rmsnorm(fn(h), post_g)
    return x + res_w * (1.0 + gate[:, None, :]) * y


def setup_inputs(seed: int = 0) -> dict:
    key = jax.random.key(seed)
    ks = jax.random.split(key, 32)
    ne = (DEPTH + 1) // 2
    no = DEPTH // 2
    f32 = jnp.float32

    def nrm(k, shape, scale):
        return jax.random.normal(k, shape, f32) * scale

    u_lam = jax.random.uniform(ks[14], (ne, W_A), f32, minval=0.9, maxval=0.999)
    sa = u_lam ** (1.0 / LRU_C)
    a_lam = jnp.log(sa) - jnp.log1p(-sa)
    return {
        "x": nrm(ks[0], (BATCH, SEQ, D_MODEL), 1.0),
        "c": nrm(ks[1], (BATCH, D_MODEL), 1.0),
        "ada_w": nrm(ks[2], (DEPTH, D_MODEL, N_SUB * 3 * D_MODEL), 0.1 * D_MODEL ** -0.5),
        "ada_b": nrm(ks[3], (DEPTH, N_SUB * 3 * D_MODEL), 0.01),
        "norm_pre": 1.0 + nrm(ks[4], (DEPTH, N_SUB, D_MODEL), 0.02),
        "norm_post": 1.0 + nrm(ks[5], (DEPTH, N_SUB, D_MODEL), 0.02),
        "ffn_w13": nrm(ks[6], (DEPTH, 2, D_MODEL, 2 * D_FF), D_MODEL ** -0.5),
        "ffn_w2": nrm(ks[7], (DEPTH, 2, D_FF, D_MODEL), D_FF ** -0.5),
        "ab_w_in": nrm(ks[8], (ne, D_MODEL, 2 * W_A + 2 * W_B), D_MODEL ** -0.5),
        "a_conv_w": nrm(ks[9], (ne, CONV_A, W_A), CONV_A ** -0.5),
        "a_conv_b": nrm(ks[10], (ne, W_A), 0.01),
        "a_gate_w": nrm(ks[11], (ne, H_A, DH_A, 2 * DH_A), DH_A ** -0.5),
        "a_gate_b": nrm(ks[12], (ne, H_A, 2 * DH_A), 0.01),
        "a_lam": a_lam,
        "b_conv_w": nrm(ks[15], (ne, CONV_B, W_B), CONV_B ** -0.5),
        "b_conv_b": nrm(ks[16], (ne, W_B), 0.01),
        "b_norm_g": 1.0 + nrm(ks[17], (ne, W_B), 0.02),
        "b_norm_b": nrm(ks[18], (ne, W_B), 0.01),
        "ab_w_out": nrm(ks[19], (ne, W_A + W_B, D_MODEL), (W_A + W_B) ** -0.5),
        "c_w_in": nrm(ks[20], (no, D_MODEL, 2 * W_C), D_MODEL ** -0.5),
        "c_b_in": nrm(ks[21], (no, 2 * W_C), 0.01),
        "c_norm_g": 1.0 + nrm(ks[22], (no, W_C), 0.02),
        "c_norm_b": nrm(ks[23], (no, W_C), 0.01),
        "c_w_s": nrm(ks[24], (no, H_C, CHUNK, CHUNK), 0.5 * CHUNK ** -0.5),
        "c_b_s": 1.0 + nrm(ks[25], (no, H_C, CHUNK), 0.01),
        "c_w_out": nrm(ks[26], (no, W_C, D_MODEL), W_C ** -0.5),
    }


def reference(x, c, ada_w, ada_b, norm_pre, norm_post, ffn_w13, ffn_w2,
              ab_w_in, a_conv_w, a_conv_b, a_gate_w, a_gate_b, a_lam,
              b_conv_w, b_conv_b, b_norm_g, b_norm_b, ab_w_out,
              c_w_in, c_b_in, c_norm_g, c_norm_b, c_w_s, c_b_s, c_w_out):
    bsz = x.shape[0]
    c_act = jax.nn.silu(c)
    for l in range(DEPTH):
        mod = (c_act @ ada_w[l] + ada_b[l]).reshape(bsz, N_SUB, 3, D_MODEL)

        def ffn_pre(h, l=l):
            return _swiglu(h, ffn_w13[l, 0], ffn_w2[l, 0])

        def ffn_post(h, l=l):
            return _swiglu(h, ffn_w13[l, 1], ffn_w2[l, 1])

        if l % 2 == 0:
            k = l // 2

            def mixer(h, k=k):
                return _mixer_ab(h, ab_w_in[k], a_conv_w[k], a_conv_b[k], a_gate_w[k],
                                 a_gate_b[k], a_lam[k], b_conv_w[k], b_conv_b[k],
                                 b_norm_g[k], b_norm_b[k], ab_w_out[k])
        else:
            k = l // 2

            def mixer(h, k=k):
                return _mixer_c(h, c_w_in[k], c_b_in[k], c_norm_g[k], c_norm_b[k],
                                c_w_s[k], c_b_s[k], c_w_out[k])

        x = _sublayer(x, ffn_pre, norm_pre[l, 0], norm_post[l, 0],
                      mod[:, 0, 0], mod[:, 0, 1], mod[:, 0, 2], 0.5)
        x = _sublayer(x, mixer, norm_pre[l, 1], norm_post[l, 1],
                      mod[:, 1, 0], mod[:, 1, 1], mod[:, 1, 2], 1.0)
        x = _sublayer(x, ffn_post, norm_pre[l, 2], norm_post[l, 2],
                      mod[:, 2, 0], mod[:, 2, 1], mod[:, 2, 2], 0.5)
    return x
```

```python
import os
import numpy as np
import concourse.bass as bass
import concourse.mybir as mybir
from concourse.bass_utils import run_bass_kernel_spmd

F32 = mybir.dt.float32
BF16 = mybir.dt.bfloat16
AF = mybir.ActivationFunctionType
ALU = mybir.AluOpType

D = 1024
DC = 8
DFF = 2816
FC = 22
WA = 512
TILE = 1024
HALO = 32
WCOL = HALO + TILE
EPS = 1e-6
NCORES = 8
PUMP_N = 5

VOFF = {}


def _mk_voff():
    o = 0

    def add(name, n):
        nonlocal o
        VOFF[name] = o
        o += n

    add("c", 8)
    for l in range(2):
        for s in range(3):
            add(("npre", l, s), 8)
    for l in range(2):
        for s in range(3):
            add(("npost", l, s), 8)
    for k in range(4):
        add(("acw", k), 4)
    add("acb", 4)
    add("gbr", 4)
    add("gbi", 4)
    add("lam", 4)
    for k in range(31):
        add(("bcw", k), 4)
    add("bcb", 4)
    add("bng", 4)
    add("bnb", 4)
    add("cbu", 8)
    add("cbv", 8)
    add("cng", 8)
    add("cnb", 8)
    add("mhalo", 1)
    add("mprev", 8)
    add("mpre", 12)
    return o


NV = _mk_voff()


class Sched:
    def __init__(self):
        self.streams = {k: [] for k in ("pe", "act", "dve", "pool", "sp")}
        self.semval = {}
        self.lastw = {}
        self.readers = {}
        self.seen = {k: {} for k in self.streams}
        self.rr = {}

    def _deps(self, reads, writes):
        deps = set()
        for r in reads:
            t = self.lastw.get(r)
            if t:
                deps.add(t)
        for w in writes:
            t = self.lastw.get(w)
            if t:
                deps.add(t)
            for t in self.readers.get(w, ()):
                deps.add(t)
        return deps

    def add(self, stream, fns, reads=(), writes=(), semkey=None, inc=1, extra=()):
        deps = self._deps(reads, writes)
        deps.update(extra)
        semkey = semkey or stream
        self.semval[semkey] = self.semval.get(semkey, 0) + inc
        tk = (semkey, self.semval[semkey])
        best = {}
        for sk, v in deps:
            if stream == "pe" and sk == "pe":
                continue
            if v > best.get(sk, 0):
                best[sk] = v
        waits = []
        for sk, v in best.items():
            if self.seen[stream].get(sk, 0) >= v:
                continue
            self.seen[stream][sk] = v
            waits.append((sk, v))
        if not isinstance(fns, (list, tuple)):
            fns = [fns]
        self.streams[stream].append((waits, list(fns), (semkey, inc)))
        for w in writes:
            self.lastw[w] = tk
            self.readers[w] = []
        for r in reads:
            self.readers.setdefault(r, []).append(tk)
        return tk

    def dma(self, stream, fn, reads=(), writes=(), semkey=None, rr=None):
        extra = ()
        if rr is not None:
            cls, R = rr
            i = self.rr.get(cls, 0)
            self.rr[cls] = i + 1
            semkey = "%s%d" % (cls, i % R)
            prev = self.lastw.get(("__sem", semkey))
            if prev:
                extra = (prev,)
        tk = self.add(stream, fn, reads, writes, semkey=semkey, inc=16, extra=extra)
        if rr is not None:
            self.lastw[("__sem", semkey)] = tk
        return tk

    def alias(self, old_keys, new_keys):
        ts = set()
        for k in old_keys:
            t = self.lastw.get(k)
            if t:
                ts.add(t)
            for t in self.readers.get(k, ()):
                ts.add(t)
        for k in new_keys:
            self.readers.setdefault(k, []).extend(ts)

    def final_wait(self, stream, keys):
        deps = self._deps(keys, ())
        best = {}
        for sk, v in deps:
            if v > best.get(sk, 0):
                best[sk] = v
        self.streams[stream].append((list(best.items()), [], None))


def MM(out, lhsT, rhs, start=True, stop=True):
    return lambda e: e.matmul(out, lhsT, rhs, start=start, stop=stop)


def TR(out, in_, ident):
    return lambda e: e.transpose(out, in_, ident)


def ACT(out, in_, func, bias=None, scale=None):
    kw = {}
    if bias is not None:
        kw["bias"] = bias
    if scale is not None:
        kw["scale"] = scale
    return lambda e: e.activation(out=out, in_=in_, func=func, **kw)


def TT(out, in0, in1, op):
    return lambda e: e.tensor_tensor(out=out, in0=in0, in1=in1, op=op)


def TS(out, in0, s1, s2, op0, op1):
    return lambda e: e.tensor_scalar(out=out, in0=in0, scalar1=s1, scalar2=s2, op0=op0, op1=op1)


def TS1(out, in0, s1, op):
    return lambda e: e.tensor_single_scalar(out=out, in_=in0, scalar=s1, op=op)


def STT(out, in0, scalar, in1, op0, op1):
    return lambda e: e.scalar_tensor_tensor(out=out, in0=in0, scalar=scalar, in1=in1, op0=op0, op1=op1)


def SCAN(out, d0, d1, init):
    return lambda e: e.tensor_tensor_scan(out=out, data0=d0, data1=d1, initial=init, op0=ALU.mult, op1=ALU.add)


def CP(out, in_):
    return lambda e: e.tensor_copy(out=out, in_=in_)


def RCP(out, in_):
    return lambda e: e.reciprocal(out=out, in_=in_)


def MSET(ap, v):
    return lambda e: e.memset(ap, v)


def DMA(out, in_):
    return lambda e: e.dma_start(out=out, in_=in_)


def build(ntok, phase, nsub=6):
    nt = ntok // TILE
    nc = bass.Bass("TRN2", target_bir_lowering=False)
    S = Sched()
    redun = phase == "F"
    doA = "A" in phase or redun
    doB = "B" in phase or redun
    fused = phase == "AB"

    def din(name, shape, dt=F32):
        return nc.dram_tensor(name, shape, dt, kind="ExternalInput").ap()

    def dout(name, shape, dt=F32):
        return nc.dram_tensor(name, shape, dt, kind="ExternalOutput").ap()

    def dint(name, shape, dt=F32):
        return nc.dram_tensor(name, shape, dt, kind="Internal").ap()

    vecs_d = din("vecs", [128, NV])
    consts_d = din("consts", [128, 384])
    adaw_d = din("adaw", [36, 128, 4096])
    adab_d = din("adab", [36, 512])
    w13_d = din("w13L", [88, 128, 2048])
    w2_d = din("w2L", [32, 128, 2816])
    if doA:
        xin_d = din("xin", [128, 8, ntok + HALO])
        abin_d = din("abinL", [8, 128, 2048])
        gatebd_d = din("gatebd", [128, 1024])
    if doB:
        about_d = din("aboutL", [8, 128, 1024])
        cin_d = din("cinL", [8, 128, 2048])
        cout_d = din("coutL", [8, 128, 1024])
        wsT_d = din("wsT", [128, 1024])
        bs_d = din("bs", [1, 1024])
        out_d = dout("out", [128, 8, ntok])
    if redun:
        xpre_d = din("xpre", [128, 8, 3 * ntok])
    elif fused:
        x1_d = dint("x1T", [128, 8, ntok])
        P_d = dint("PT", [128, 4, ntok])
        Q_d = dint("QT", [128, 4, ntok])
        yb_d = dint("ybT", [128, 4, ntok])
        st_d = dint("st", [128, 8])
        stall_d = dint("stall", [NCORES * 128, 8])
    else:
        mk = dout if doA else din
        x1_d = mk("x1T", [128, 8, ntok])
        P_d = mk("PT", [128, 4, ntok])
        Q_d = mk("QT", [128, 4, ntok])
        yb_d = mk("ybT", [128, 4, ntok])
        if doA:
            st_d = dout("st", [128, 8])
        else:
            stall_d = din("stall", [NCORES * 128, 8])

    import contextlib
    es = contextlib.ExitStack()
    with es:
        def sb(name, shape, dt=F32):
            return es.enter_context(nc.sbuf_tensor("s_" + name, shape, dt))

        xres = sb("xres", [128, 8, WCOL])
        hb = sb("hb", [128, 8, WCOL], BF16)
        act = sb("act", [128, FC, WCOL], BF16)
        ybuf = sb("ybuf", [128, 16, 512])
        yhalo = sb("yhalo", [128, 8, HALO])
        w13s = [sb("w13s%d" % i, [128, 8, 256], BF16) for i in range(2)]
        w2s = [sb("w2s%d" % i, [128, FC, 128], BF16) for i in range(2)]
        sqb = [sb("sqb%d" % i, [128, 512], BF16) for i in range(4)]
        rstd = sb("rstd", [128, WCOL])
        rstd2 = sb("rstd2", [128, WCOL])
        tmpf = [sb("tmpf%d" % i, [128, 512]) for i in range(3)]
        vecs = sb("vecs", [128, NV])
        modT = sb("modT", [128, 144])
        gsT = sb("gsT", [128, 48])
        coefT = sb("coefT", [128, 48])
        cst = sb("cst", [128, 4])
        ones_b = sb("ones_b", [128, 128], BF16)
        one_f = sb("one_f", [1, 1])
        cact = sb("cact", [128, 8])
        cact_b = sb("cact_b", [128, 8], BF16)
        if doA:
            AXH = 4
            AX = [sb("AX%d" % i, [128, AXH + TILE] if redun else [128, HALO + 512]) for i in range(4)]
            VB = [sb("VB%d" % i, [128, HALO + 512]) for i in range(4)] if not redun else None
            vhalo = sb("vhalo", [128, 4, HALO], BF16) if redun else None
            gate_b = sb("gate_b", [128, 8, 128], BF16)
            bd64_b = sb("bd64_b", [128, 128], BF16)
            zeros = sb("zeros", [128, 512]) if not redun else None
            hcar = sb("hcar", [128, 4])
            acar = sb("acar", [128, 4])
            c8 = sb("c8", [128, 4])
            c16 = sb("c16", [128, 4])
            lt = [sb("lt%d" % i, [128, 4]) for i in range(2)]
            xrb = [sb("xrb%d" % i, [128, 512], BF16) for i in range(4)]
        if doB:
            ident_b = sb("ident_b", [128, 128], BF16)
            tril = sb("tril", [128, 128])
            wsT_b = sb("wsT_b", [128, 8, 128], BF16)
            bsb = sb("bsb", [128, 8, 128])
            if not redun:
                stall = sb("stall_sb", [128, NCORES, 8])
                carry = sb("carry", [128, 4])
                ctmp = [sb("ctmp%d" % i, [128, 4]) for i in range(2)]
            vTb = [sb("vTb%d" % i, [128, 512], BF16) for i in range(2)]
        ps = [es.enter_context(nc.psum_tensor("ps%d" % i, [128, 512], F32)) for i in range(8)]
        if os.environ.get("KERNEL_PLAN_ONLY"):
            print("SBUF bytes/partition remaining:", nc.sbuf_bytes_remaining)
        sem_names = ["pe", "act", "dve", "pool", "xload", "w13s0", "w13s1", "w2s0", "w2s1",
                     "misc", "cc"] + ["st%d" % i for i in range(8)] + ["ld%d" % i for i in range(8)] + ["lq%d" % i for i in range(4)] + ["ada0", "ada1"]
        sems = {n: es.enter_context(nc.semaphore(n)) for n in sem_names}

        psrot = {"pool": [0, 1, 2, 3, 4], "i": 0}

        def nextps():
            i = psrot["pool"][psrot["i"] % len(psrot["pool"])]
            psrot["i"] += 1
            return i

        V = lambda name, n=1: vecs[:, VOFF[name]:VOFF[name] + n]
        Vc = lambda name, c: vecs[:, VOFF[name] + c:VOFF[name] + c + 1]
        EPSAP = cst[:, 0:1]
        ONEAP = cst[:, 1:2]

        rot = {}

        def rotate(name, lst):
            i = rot.get(name, 0)
            rot[name] = i + 1
            return i % len(lst)

        S.dma("sp", DMA(vecs[:], vecs_d[:, :]), writes=["vecs"], semkey="misc")
        S.add("dve", MSET(cst[:, 0:1], EPS), writes=["cst"])
        S.add("dve", MSET(cst[:, 1:2], 1.0), writes=["cst"])
        S.add("dve", MSET(ones_b[:], 1.0), writes=["ones_b"])
        S.add("dve", MSET(one_f[:], 1.0), writes=["one_f"])
        if doA:
            if redun:
                for c_ in range(4):
                    S.add("dve", MSET(AX[c_][:, 0:AXH], 0.0), writes=[("AX", c_, "h")])
            else:
                S.add("dve", MSET(zeros[:], 0.0), writes=["zeros"])
            S.add("dve", MSET(hcar[:], 0.0), writes=["hcar"])
            S.add("dve", MSET(acar[:], 1.0), writes=["acar"])
            S.dma("pool", DMA(gate_b[:], gatebd_d.rearrange("p (a b) -> p a b", a=8)), writes=["gate_b"], rr=("lq", 4))
            S.dma("pool", DMA(bd64_b[:], consts_d[:, 256:384]), writes=["bd64_b"], rr=("lq", 4))
            ev, tv = lt
            S.add("act", ACT(ev[:], V("lam", 4), AF.Exp, scale=-1.0), reads=["vecs"], writes=["lt0"])
            S.add("dve", TS(tv[:], ev[:], 0.2, -0.25, ALU.mult, ALU.add), reads=["lt0"], writes=["lt1"])
            for cc in (1.0 / 3.0, -0.5, 1.0):
                S.add("dve", TT(tv[:], tv[:], ev[:], ALU.mult), reads=["lt0", "lt1"], writes=["lt1"])
                S.add("dve", TS1(tv[:], tv[:], cc, ALU.add), reads=["lt1"], writes=["lt1"])
            S.add("dve", TT(tv[:], tv[:], ev[:], ALU.mult), reads=["lt0", "lt1"], writes=["lt1"])
            S.add("dve", TS1(c8[:], tv[:], -8.0, ALU.mult), reads=["lt1"], writes=["c8"])
            S.add("dve", TS1(c16[:], tv[:], -16.0, ALU.mult), reads=["lt1"], writes=["c16"])
        if doB:
            S.dma("pool", DMA(ident_b[:], consts_d[:, 0:128]), writes=["ident_b"], rr=("lq", 4))
            S.dma("sp", DMA(tril[:], consts_d[:, 128:256]), writes=["tril"], rr=("ld", 8))
            S.dma("sp", DMA(ybuf[:, 0:2, :], wsT_d.rearrange("p (a b) -> p a b", a=2)), writes=[("yb", 0), ("yb", 1)], rr=("ld", 8))
            for h in range(8):
                S.add("dve", TT(wsT_b[:, h, :], ybuf[:, h // 4, (h % 4) * 128:(h % 4 + 1) * 128], tril[:], ALU.mult),
                      reads=[("yb", h // 4), "tril"], writes=["wsT_b"])
            S.dma("sp", DMA(bsb[:], bs_d.partition_broadcast(128).rearrange("p o (a b) -> p (o a) b", a=8)), writes=["bsb"], rr=("ld", 8))

        S.add("act", ACT(cact[:], V("c", 8), AF.Silu), reads=["vecs"], writes=["cact"])
        S.add("dve", CP(cact_b[:], cact[:]), reads=["cact"], writes=["cact_b"])
        ada_st = [act[:, 0:4, :].rearrange("p a b -> p (a b)")[:, 0:4096].rearrange("p (k n) -> p k n", k=8),
                  act[:, 4:8, :].rearrange("p a b -> p (a b)")[:, 0:4096].rearrange("p (k n) -> p k n", k=8)]
        layers = [0, 1] if doB else [0]
        for l in layers:
            for ct in range(18):
                if (not doB) and ct >= 12:
                    continue
                si = rotate("ada", ada_st)
                slot = ada_st[si]
                skey = ("adast", si)
                for hf in range(2):
                    S.dma("pool", DMA(slot[:, hf * 4:(hf + 1) * 4, :],
                                      adaw_d[l * 18 + ct, :, hf * 2048:(hf + 1) * 2048].rearrange("p (k n) -> p k n", k=4)),
                          writes=[skey], semkey="ada%d" % si)
                bi = rotate("brow", [0, 1])
                browt, kbrow = tmpf[bi][0:1, :], ("tmpf", bi)
                S.dma("sp", DMA(browt, adab_d[l * 18 + ct:l * 18 + ct + 1, :]), writes=[kbrow], rr=("ld", 8))
                pi = nextps()
                S.add("pe", [MM(ps[pi][0:1, :], cact_b[:, k:k + 1], slot[:, k, :], start=(k == 0), stop=(k == 7)) for k in range(8)],
                      reads=[skey, "cact_b"], writes=[("ps", pi)])
                mi = 2
                mrowt, kmrow = tmpf[mi][0:1, :], ("tmpf", mi)
                S.add("dve", TT(mrowt, ps[pi][0:1, :], browt, ALU.add), reads=[("ps", pi), kbrow], writes=[kmrow])
                pj = nextps()
                S.add("pe", [MM(ps[pj][:, q:q + 1], mrowt[:, q * 128:(q + 1) * 128], one_f[0:1, 0:1]) for q in range(4)],
                      reads=[kmrow, "one_f"], writes=[("ps", pj)])
                S.add("dve", CP(modT[:, l * 72 + ct * 4:l * 72 + ct * 4 + 4], ps[pj][:, 0:4]), reads=[("ps", pj)], writes=["modT"])
        S.alias([("adast", 0), ("adast", 1)], [("act", fc, j) for fc in range(FC) for j in range(3)])

        def modcol(l, sub, kind):
            return l * 72 + (sub * 3 + kind) * 8

        for l in layers:
            for sub in range(3):
                o = (l * 3 + sub) * 8
                m1 = modcol(l, sub, 1)
                m2 = modcol(l, sub, 2)
                S.add("dve", STT(gsT[:, o:o + 8], modT[:, m1:m1 + 8], 1.0, V(("npre", l, sub), 8), ALU.add, ALU.mult),
                      reads=["modT", "vecs"], writes=["gsT"])
                S.add("dve", STT(coefT[:, o:o + 8], modT[:, m2:m2 + 8], 1.0, V(("npost", l, sub), 8), ALU.add, ALU.mult),
                      reads=["modT", "vecs"], writes=["coefT"])
                if sub != 1:
                    S.add("dve", TS1(coefT[:, o:o + 8], coefT[:, o:o + 8], 0.5, ALU.mult), reads=["coefT"], writes=["coefT"])

        def jkey(c0):
            return 0 if c0 == 0 else (1 if c0 == HALO else 2)

        def prenorm(l, sub, subtiles):
            o = (l * 3 + sub) * 8
            sh = modcol(l, sub, 0)
            for (c0, n) in subtiles:
                j = jkey(c0)
                sp_i = 5 + j
                for c in range(8):
                    qi = rotate("sqb", sqb)
                    S.add("act", ACT(sqb[qi][:, :n], xres[:, c, c0:c0 + n], AF.Square), reads=[("xres", c, j)], writes=[("sqb", qi)])
                    S.add("pe", MM(ps[sp_i][:, :n], ones_b[:], sqb[qi][:, :n], start=(c == 0), stop=(c == 7)),
                          reads=[("sqb", qi), "ones_b"], writes=[("ps", sp_i)])
                S.add("act", ACT(rstd[:, c0:c0 + n], ps[sp_i][:, :n], AF.Sqrt, bias=EPSAP, scale=1.0 / D),
                      reads=[("ps", sp_i), "cst"], writes=[("rstd", j)])
                S.add("dve", RCP(rstd[:, c0:c0 + n], rstd[:, c0:c0 + n]), reads=[("rstd", j)], writes=[("rstd", j)])
                for c in range(8):
                    ti = rotate("tmpf", tmpf)
                    S.add("dve", TT(tmpf[ti][:, :n], xres[:, c, c0:c0 + n], rstd[:, c0:c0 + n], ALU.mult),
                          reads=[("xres", c, j), ("rstd", j)], writes=[("tmpf", ti)])
                    S.add("act", ACT(hb[:, c, c0:c0 + n], tmpf[ti][:, :n], AF.Identity, bias=modT[:, sh + c:sh + c + 1],
                                     scale=gsT[:, o + c:o + c + 1]),
                          reads=[("tmpf", ti), "modT", "gsT"], writes=[("hb", c, j)])

        def ydst(dc, c0, n):
            if c0 == 0:
                return yhalo[:, dc, 0:n], ("yh", dc)
            jj = 0 if c0 == HALO else 1
            return ybuf[:, dc * 2 + jj, 0:n], ("yb", dc * 2 + jj)

        def post_evac(pi, dc, c0, n):
            j = jkey(c0)
            sp_i = 5 + j
            dst, dkey = ydst(dc, c0, n)
            S.add("act", ACT(dst, ps[pi][:, :n], AF.Copy), reads=[("ps", pi)], writes=[dkey])
            qi = rotate("sqb", sqb)
            S.add("act", ACT(sqb[qi][:, :n], ps[pi][:, :n], AF.Square), reads=[("ps", pi)], writes=[("sqb", qi)])
            flush_stats()
            pend_stats.append((sp_i, qi, n, dc))

        pend_stats = []

        def flush_stats():
            while pend_stats:
                sp_i, qi, n, dc = pend_stats.pop(0)
                S.add("pe", MM(ps[sp_i][:, :n], ones_b[:], sqb[qi][:, :n], start=(dc == 0), stop=(dc == 7)),
                      reads=[("sqb", qi), "ones_b"], writes=[("ps", sp_i)])

        def post_apply(l, sub, subtiles):
            flush_stats()
            o = (l * 3 + sub) * 8
            for (c0, n) in subtiles:
                j = jkey(c0)
                sp_i = 5 + j
                S.add("act", ACT(rstd2[:, c0:c0 + n], ps[sp_i][:, :n], AF.Sqrt, bias=EPSAP, scale=1.0 / D),
                      reads=[("ps", sp_i), "cst"], writes=[("rstd2", j)])
                S.add("dve", RCP(rstd2[:, c0:c0 + n], rstd2[:, c0:c0 + n]), reads=[("rstd2", j)], writes=[("rstd2", j)])
                for dc in range(8):
                    src, skey = ydst(dc, c0, n)
                    S.add("dve", TT(src, src, rstd2[:, c0:c0 + n], ALU.mult), reads=[skey, ("rstd2", j)], writes=[skey])
                    S.add("dve", STT(xres[:, dc, c0:c0 + n], src, coefT[:, o + dc:o + dc + 1], xres[:, dc, c0:c0 + n], ALU.mult, ALU.add),
                          reads=[skey, "coefT", ("xres", dc, j)], writes=[("xres", dc, j)])

        def load_w13(src_ap):
            si = rotate("w13s", w13s)
            S.dma("pool", DMA(w13s[si][:], src_ap.rearrange("p (k n) -> p k n", k=8)), writes=[("w13s", si)], semkey="w13s%d" % si)
            return si

        def load_w2(src_ap, nk):
            si = rotate("w2s", w2s)
            h = nk // 2
            for a, b in ((0, h), (h, nk)):
                S.dma("pool", DMA(w2s[si][:, a:b, :], src_ap[:, a * 128:b * 128].rearrange("p (k n) -> p k n", n=128)),
                      writes=[("w2s", si)], semkey="w2s%d" % si)
            return si

        def proj8(si, half, c0, n, pi=None):
            j = jkey(c0)
            pi = nextps() if pi is None else pi
            S.add("pe", [MM(ps[pi][:, :n], w13s[si][:, k, half * 128:(half + 1) * 128], hb[:, k, c0:c0 + n],
                            start=(k == 0), stop=(k == 7)) for k in range(8)],
                  reads=[("w13s", si)] + [("hb", k, j) for k in range(8)], writes=[("ps", pi)])
            return pi

        def ffn(l, f, subtiles, pre=True, post=True):
            sub = 0 if f == 0 else 2
            if pre:
                prenorm(l, sub, subtiles)
            for fc in range(FC):
                si = load_w13(w13_d[(l * 2 + f) * 22 + fc])
                for (c0, n) in subtiles:
                    j = jkey(c0)
                    gi = proj8(si, 0, c0, n)
                    ui = proj8(si, 1, c0, n)
                    ti = rotate("tmpf", tmpf)
                    S.add("act", ACT(tmpf[ti][:, :n], ps[gi][:, :n], AF.Silu), reads=[("ps", gi)], writes=[("tmpf", ti)])
                    S.add("dve", TT(act[:, fc, c0:c0 + n], tmpf[ti][:, :n], ps[ui][:, :n], ALU.mult),
                          reads=[("tmpf", ti), ("ps", ui)], writes=[("act", fc, j)])
                    pump(PUMP_N)
            drain()
            for dc in range(8):
                si = load_w2(w2_d[(l * 2 + f) * 8 + dc], FC)
                for (c0, n) in subtiles:
                    j = jkey(c0)
                    pi = nextps()
                    S.add("pe", [MM(ps[pi][:, :n], w2s[si][:, k, :], act[:, k, c0:c0 + n], start=(k == 0), stop=(k == FC - 1))
                                 for k in range(FC)],
                          reads=[("w2s", si)] + [("act", k, j) for k in range(FC)], writes=[("ps", pi)])
                    post_evac(pi, dc, c0, n)
            if post:
                post_apply(l, sub, subtiles)

        def trans(lp, subp, ln, subn, subtiles):
            for st in subtiles:
                post_apply(lp, subp, [st])
                if ln is not None:
                    prenorm(ln, subn, [st])

        def out_proj(src_d, l, subtiles, src=None, srckey="hb", post=True):
            src = hb if src is None else src
            for dc in range(8):
                si = load_w2(src_d[dc], 8)
                for (c0, n) in subtiles:
                    j = jkey(c0)
                    pi = nextps()
                    S.add("pe", [MM(ps[pi][:, :n], w2s[si][:, k, :], src[:, k, c0:c0 + n], start=(k == 0), stop=(k == 7))
                                 for k in range(8)],
                          reads=[("w2s", si)] + [(srckey, k, j) for k in range(8)], writes=[("ps", pi)])
                    post_evac(pi, dc, c0, n)
            if post:
                post_apply(l, 1, subtiles)

        def blk(i):
            return ybuf[:, i, :], ("yb", i)

        import collections
        bgq = collections.deque()

        def pump(k):
            while k > 0 and bgq:
                try:
                    next(bgq[0])
                    k -= 1
                except StopIteration:
                    bgq.popleft()

        def drain():
            pump(1 << 30)

        def mixer_a_fast(t, subtiles, mode):
            drain()
            mcol = Vc("mpre", t) if mode == "prefix" else None
            psrot["pool"] = [0, 1, 2, 3] if mode == "direct" else [0, 1, 2, 3, 4, 5, 6, 7]

            def axr(c, jj):
                return AX[c][:, AXH + jj * 512:AXH + (jj + 1) * 512], ("AX", c, jj)

            def scr(c, jj):
                if mode == "prefix":
                    return axr(c, jj)
                return blk(c * 4 + (1 - jj) * 2)

            def project(c0, n, cs=range(4), want_gate=False):
                agi = {}
                j = jkey(c0)
                for c in cs:
                    si = load_w13(abin_d[c])
                    axi = proj8(si, 0, c0, n)
                    if c0 == 0:
                        S.add("dve", TS1(AX[c][:, 0:AXH], ps[axi][:, HALO - AXH:HALO], V("mhalo"), ALU.mult),
                              reads=[("ps", axi), "vecs"], writes=[("AX", c, "h")])
                        continue
                    if want_gate:
                        agi[c] = proj8(si, 1, c0, n, pi=4 + c)
                    reg, kreg = axr(c, j - 1)
                    S.add("act", ACT(reg, ps[axi][:, :], AF.Copy), reads=[("ps", axi)], writes=[kreg])
                return agi

            def chains(lanes, agi=None, c0n=None):
                BA = {ln: blk(ln[0] * 4 + ln[1] * 2) for ln in lanes}
                BB = {ln: blk(ln[0] * 4 + ln[1] * 2 + 1) for ln in lanes}
                def convkeys(c, jj):
                    return [("AX", c, jj), ("AX", c, jj - 1) if jj > 0 else ("AX", c, "h")]
                for (c, jj) in lanes:
                    a0 = AXH + jj * 512
                    XR, kXR = BA[(c, jj)]
                    S.add("dve", TS(XR, AX[c][:, a0:a0 + 512], Vc(("acw", 3), c), Vc("acb", c), ALU.mult, ALU.add),
                          reads=convkeys(c, jj) + ["vecs"], writes=[kXR])
                for k in range(3):
                    for (c, jj) in lanes:
                        a0 = AXH + jj * 512
                        XR, kXR = BA[(c, jj)]
                        S.add("dve", STT(XR, AX[c][:, a0 - 3 + k:a0 - 3 + k + 512], Vc(("acw", k), c), XR, ALU.mult, ALU.add),
                              reads=convkeys(c, jj) + ["vecs", kXR], writes=[kXR])
                for (c, jj) in lanes:
                    if jj == 1:
                        if mode == "prefix":
                            S.add("dve", TS1(AX[c][:, 0:AXH], AX[c][:, TILE:TILE + AXH], mcol, ALU.mult),
                                  reads=[("AX", c, 1), "vecs"], writes=[("AX", c, "h")])
                        else:
                            S.add("dve", CP(AX[c][:, 0:AXH], AX[c][:, TILE:TILE + AXH]), reads=[("AX", c, 1)], writes=[("AX", c, "h")])
                for h0 in range(0, len(lanes), 4):
                    grp = lanes[h0:h0 + 4]
                    xis = {}
                    for ln in grp:
                        xi = rotate("xrb", xrb)
                        xis[ln] = xi
                        S.add("act", ACT(xrb[xi][:], BA[ln][0], AF.Copy), reads=[BA[ln][1]], writes=[("xrb", xi)])
                    for ln in grp:
                        c, jj = ln
                        reg, kreg = scr(c, jj)
                        ri = nextps()
                        S.add("pe", MM(ps[ri][:, :], gate_b[:, c * 2, :], xrb[xis[ln]][:]), reads=[("xrb", xis[ln]), "gate_b"], writes=[("ps", ri)])
                        S.add("act", ACT(BB[ln][0], ps[ri][:, :], AF.Sigmoid, bias=Vc("gbr", c)), reads=[("ps", ri), "vecs"], writes=[BB[ln][1]])
                        ii = nextps()
                        S.add("pe", MM(ps[ii][:, :], gate_b[:, c * 2 + 1, :], xrb[xis[ln]][:]), reads=[("xrb", xis[ln]), "gate_b"], writes=[("ps", ii)])
                        S.add("act", ACT(reg, ps[ii][:, :], AF.Sigmoid, bias=Vc("gbi", c)), reads=[("ps", ii), "vecs"], writes=[kreg])
                for ln in lanes:
                    c, jj = ln
                    reg, kreg = scr(c, jj)
                    if mode == "prefix":
                        S.add("dve", STT(BA[ln][0], reg, mcol, BA[ln][0], ALU.mult, ALU.mult), reads=[kreg, BA[ln][1], "vecs"], writes=[BA[ln][1]])
                    else:
                        S.add("dve", TT(BA[ln][0], reg, BA[ln][0], ALU.mult), reads=[kreg, BA[ln][1]], writes=[BA[ln][1]])
                for ln in lanes:
                    c, jj = ln
                    reg, kreg = scr(c, jj)
                    S.add("act", ACT(reg, BB[ln][0], AF.Exp, scale=c8[:, c:c + 1]), reads=[BB[ln][1], "c8"], writes=[kreg])
                    S.add("act", ACT(BB[ln][0], BB[ln][0], AF.Exp, scale=c16[:, c:c + 1]), reads=[BB[ln][1], "c16"], writes=[BB[ln][1]])
                for ln in lanes:
                    S.add("act", ACT(BB[ln][0], BB[ln][0], AF.Sqrt, bias=ONEAP, scale=-1.0), reads=[BB[ln][1], "cst"], writes=[BB[ln][1]])
                for ln in lanes:
                    S.add("dve", TT(BB[ln][0], BB[ln][0], BA[ln][0], ALU.mult), reads=[BB[ln][1], BA[ln][1]], writes=[BB[ln][1]])
                for ln in sorted(lanes):
                    c, jj = ln
                    reg, kreg = scr(c, jj)
                    Hh, kH = BA[ln]
                    S.add("dve", SCAN(Hh, reg, BB[ln][0], hcar[:, c:c + 1]), reads=[kreg, BB[ln][1], "hcar"], writes=[kH])
                    S.add("dve", CP(hcar[:, c:c + 1], Hh[:, 511:512]), reads=[kH], writes=["hcar"])
                if mode == "direct":
                    c0, n = c0n
                    for ln in lanes:
                        c, jj = ln
                        reg, kreg = scr(c, jj)
                        S.add("act", ACT(reg, ps[agi[c]][:, :], AF.Gelu_apprx_tanh), reads=[("ps", agi[c])], writes=[kreg])
                        S.add("dve", TT(act[:, c, c0:c0 + n], BA[ln][0], reg, ALU.mult), reads=[BA[ln][1], kreg], writes=[("act", c, jj + 1)])

            if mode == "prefix":
                for (c0, n) in subtiles:
                    project(c0, n)
                chains([(c, jj) for jj in range(2) for c in range(4)])
            else:
                for (c0, n) in subtiles:
                    agi = project(c0, n, want_gate=True)
                    if c0 == 0:
                        continue
                    chains([(c, jkey(c0) - 1) for c in range(4)], agi=agi, c0n=(c0, n))
            psrot["pool"] = [0, 1, 2, 3, 4]

        def mixer_b_fast(t, subtiles):
            free_keys = [("act", fc, j) for fc in range(8, FC) for j in range(3)]
            new_keys = [("dg", 0), ("dg", 1)] + [("VBb", c) for c in range(4)]
            S.alias(free_keys, new_keys)
            dgv = act[:, 8:16, :].rearrange("p a b -> p (a b)")[:, 0:7936].rearrange("p (s k n) -> p s k n", s=2, k=31)
            vbv = act[:, 16:22, :].rearrange("p a b -> p (a b)")[:, 0:4 * WCOL].rearrange("p (c n) -> p c n", c=4)
            for c in range(4):
                si = load_w13(abin_d[4 + c])
                kVB = ("VBb", c)
                if t > 0:
                    S.add("dve", CP(vbv[:, c, 0:HALO], vhalo[:, c, :]), reads=["vhalo"], writes=[kVB])
                for (c0, n) in subtiles:
                    bvi = proj8(si, 0, c0, n)
                    bgi = proj8(si, 1, c0, n)
                    ti = rotate("tmpf", tmpf)
                    SG, kSG = tmpf[ti], ("tmpf", ti)
                    S.add("act", ACT(SG[:, :n], ps[bgi][:, :n], AF.Sigmoid), reads=[("ps", bgi)], writes=[kSG])
                    if c0 == 0:
                        S.add("dve", TT(SG[:, :HALO], ps[bvi][:, :HALO], SG[:, :HALO], ALU.mult), reads=[("ps", bvi), kSG], writes=[kSG])
                        S.add("dve", TS1(vbv[:, c, 0:HALO], SG[:, :HALO], V("mhalo"), ALU.mult), reads=[kSG, "vecs"], writes=[kVB])
                    else:
                        S.add("dve", TT(vbv[:, c, c0:c0 + n], ps[bvi][:, :n], SG[:, :n], ALU.mult), reads=[("ps", bvi), kSG], writes=[kVB])
                ds = c % 2
                for k in range(31):
                    if k % 2 == 0:
                        S.add("dve", TS1(dgv[:, ds, k, :], ident_b[:], Vc(("bcw", k), c), ALU.mult), reads=["ident_b", "vecs"], writes=[("dg", ds)])
                    else:
                        S.add("act", ACT(dgv[:, ds, k, :], ident_b[:], AF.Identity, scale=Vc(("bcw", k), c)), reads=["ident_b", "vecs"], writes=[("dg", ds)])
                for (c0, n) in subtiles:
                    if c0 == 0:
                        continue
                    j = jkey(c0)
                    jj = j - 1
                    cvi = nextps()
                    S.add("pe", [MM(ps[cvi][:, :n], dgv[:, ds, k, :], vbv[:, c, c0 - 30 + k:c0 - 30 + k + n], start=(k == 0), stop=(k == 30))
                                 for k in range(31)], reads=[("dg", ds), kVB], writes=[("ps", cvi)])
                    CV, kCV = blk(1 + jj * 8)
                    MN, kMN = blk(3 + jj * 8)
                    VR, kVR = blk(4 + jj * 8)
                    S.add("act", ACT(CV, ps[cvi][:, :n], AF.Identity, bias=Vc("bcb", c)), reads=[("ps", cvi), "vecs"], writes=[kCV])
                    q1 = rotate("sqb", sqb)
                    S.add("act", ACT(sqb[q1][:], CV, AF.Copy), reads=[kCV], writes=[("sqb", q1)])
                    mi_ = nextps()
                    S.add("pe", MM(ps[mi_][:, :], bd64_b[:], sqb[q1][:]), reads=[("sqb", q1), "bd64_b"], writes=[("ps", mi_)])
                    q2 = rotate("sqb", sqb)
                    S.add("act", ACT(sqb[q2][:], CV, AF.Square), reads=[kCV], writes=[("sqb", q2)])
                    vi_ = nextps()
                    S.add("pe", MM(ps[vi_][:, :], bd64_b[:], sqb[q2][:]), reads=[("sqb", q2), "bd64_b"], writes=[("ps", vi_)])
                    S.add("act", ACT(MN, ps[mi_][:, :], AF.Copy, scale=1.0 / 64), reads=[("ps", mi_)], writes=[kMN])
                    S.add("dve", TT(VR, MN, MN, ALU.mult), reads=[kMN], writes=[kVR])
                    S.add("dve", STT(VR, ps[vi_][:, :], 1.0 / 64, VR, ALU.mult, ALU.subtract), reads=[("ps", vi_), kVR], writes=[kVR])
                    S.add("act", ACT(VR, VR, AF.Sqrt, bias=EPSAP, scale=1.0), reads=[kVR, "cst"], writes=[kVR])
                    S.add("dve", RCP(VR, VR), reads=[kVR], writes=[kVR])
                    S.add("dve", TT(CV, CV, MN, ALU.subtract), reads=[kCV, kMN], writes=[kCV])
                    S.add("dve", TT(CV, CV, VR, ALU.mult), reads=[kCV, kVR], writes=[kCV])
                    S.add("act", ACT(act[:, 4 + c, c0:c0 + n], CV, AF.Silu, bias=Vc("bnb", c), scale=Vc("bng", c)),
                          reads=[kCV, "vecs"], writes=[("act", 4 + c, j)])
                S.add("dve", CP(vhalo[:, c, :], vbv[:, c, TILE:TILE + HALO]), reads=[kVB], writes=["vhalo"])
            S.alias(new_keys, free_keys)

        def mixer_ab_in(t, subtiles, mode="spill", pre=True):
            tok0 = t * TILE
            mcol = Vc("mpre", t) if mode == "prefix" else None
            if pre:
                prenorm(0, 1, subtiles)
            if mode != "spill":
                mixer_a_fast(t, subtiles, mode)
            for c in (range(4) if mode == "spill" else ()):
                si = load_w13(abin_d[c])
                for (c0, n) in subtiles:
                    axi = proj8(si, 0, c0, n)
                    if c0 == 0:
                        S.add("dve", TS1(AX[c][:, 0:HALO], ps[axi][:, 0:HALO], V("mhalo"), ALU.mult),
                              reads=[("ps", axi), "vecs"], writes=[("AX", c)])
                        continue
                    agi = proj8(si, 1, c0, n) if mode != "prefix" else None
                    jj = jkey(c0) - 1
                    XR, kXR = blk(0 + jj * 8)
                    Rb, kR = blk(1 + jj * 8)
                    Ib, kI = blk(2 + jj * 8)
                    Aa, kA = blk(3 + jj * 8)
                    Sq, kS = blk(4 + jj * 8)
                    Hh, kH = blk(5 + jj * 8)
                    At, kAt = blk(6 + jj * 8)
                    GL, kG = blk(7 + jj * 8)
                    AXc = AX[c]
                    kAX = ("AX", c)
                    S.add("act", ACT(AXc[:, HALO:HALO + 512], ps[axi][:, :], AF.Copy), reads=[("ps", axi)], writes=[kAX])
                    S.add("dve", TS(XR, AXc[:, HALO:HALO + 512], Vc(("acw", 3), c), Vc("acb", c), ALU.mult, ALU.add),
                          reads=[kAX, "vecs"], writes=[kXR])
                    for k in range(3):
                        S.add("dve", STT(XR, AXc[:, HALO - 3 + k:HALO - 3 + k + 512], Vc(("acw", k), c), XR, ALU.mult, ALU.add),
                              reads=[kAX, "vecs", kXR], writes=[kXR])
                    if mode == "prefix":
                        S.add("pool", TS1(AXc[:, 0:HALO], AXc[:, 512:512 + HALO], mcol, ALU.mult), reads=[kAX, "vecs"], writes=[kAX])
                    else:
                        S.add("pool", CP(AXc[:, 0:HALO], AXc[:, 512:512 + HALO]), reads=[kAX], writes=[kAX])
                    xi = rotate("xrb", xrb)
                    S.add("act", ACT(xrb[xi][:], XR, AF.Copy), reads=[kXR], writes=[("xrb", xi)])
                    ri = nextps()
                    S.add("pe", MM(ps[ri][:, :], gate_b[:, c * 2, :], xrb[xi][:]), reads=[("xrb", xi), "gate_b"], writes=[("ps", ri)])
                    ii = nextps()
                    S.add("pe", MM(ps[ii][:, :], gate_b[:, c * 2 + 1, :], xrb[xi][:]), reads=[("xrb", xi), "gate_b"], writes=[("ps", ii)])
                    S.add("act", ACT(Rb, ps[ri][:, :], AF.Sigmoid, bias=Vc("gbr", c)), reads=[("ps", ri), "vecs"], writes=[kR])
                    S.add("act", ACT(Ib, ps[ii][:, :], AF.Sigmoid, bias=Vc("gbi", c)), reads=[("ps", ii), "vecs"], writes=[kI])
                    S.add("act", ACT(Aa, Rb, AF.Exp, scale=c8[:, c:c + 1]), reads=[kR, "c8"], writes=[kA])
                    S.add("act", ACT(Sq, Rb, AF.Exp, scale=c16[:, c:c + 1]), reads=[kR, "c16"], writes=[kS])
                    S.add("act", ACT(Sq, Sq, AF.Sqrt, bias=ONEAP, scale=-1.0), reads=[kS, "cst"], writes=[kS])
                    S.add("pool", TT(Sq, Sq, Ib, ALU.mult), reads=[kS, kI], writes=[kS])
                    if mode == "prefix":
                        S.add("dve", STT(Sq, Sq, mcol, XR, ALU.mult, ALU.mult), reads=[kS, kXR, "vecs"], writes=[kS])
                    else:
                        S.add("pool", TT(Sq, Sq, XR, ALU.mult), reads=[kS, kXR], writes=[kS])
                    S.add("dve", SCAN(Hh, Aa, Sq, hcar[:, c:c + 1]), reads=[kA, kS, "hcar"], writes=[kH])
                    S.add("dve", CP(hcar[:, c:c + 1], Hh[:, 511:512]), reads=[kH], writes=["hcar"])
                    if mode == "prefix":
                        continue
                    S.add("act", ACT(GL, ps[agi][:, :], AF.Gelu_apprx_tanh), reads=[("ps", agi)], writes=[kG])
                    if mode == "direct":
                        S.add("pool", TT(act[:, c, c0:c0 + n], Hh, GL, ALU.mult), reads=[kH, kG], writes=[("act", c, jkey(c0))])
                        continue
                    S.add("dve", SCAN(At, Aa, zeros[:], acar[:, c:c + 1]), reads=[kA, "zeros", "acar"], writes=[kAt])
                    S.add("dve", CP(acar[:, c:c + 1], At[:, 511:512]), reads=[kAt], writes=["acar"])
                    S.add("pool", TT(Hh, Hh, GL, ALU.mult), reads=[kH, kG], writes=[kH])
                    S.add("pool", TT(At, At, GL, ALU.mult), reads=[kAt, kG], writes=[kAt])
                    d0 = tok0 + c0 - HALO
                    S.dma("sp", DMA(P_d[:, c, d0:d0 + 512], Hh), reads=[kH], writes=[("dP", c, t)], rr=("st", 8))
                    S.dma("sp", DMA(Q_d[:, c, d0:d0 + 512], At), reads=[kAt], writes=[("dQ", c, t)], rr=("st", 8))
            if mode == "prefix":
                return
            if mode == "direct":
                mixer_b_fast(t, subtiles)
                return
            for c in range(4):
                si = load_w13(abin_d[4 + c])
                for (c0, n) in subtiles:
                    bvi = proj8(si, 0, c0, n)
                    bgi = proj8(si, 1, c0, n)
                    jj = max(jkey(c0) - 1, 0)
                    SG, kSG = blk(0 + jj * 8)
                    CV, kCV = blk(1 + jj * 8)
                    CV2, kCV2 = blk(2 + jj * 8)
                    MN, kMN = blk(3 + jj * 8)
                    VR, kVR = blk(4 + jj * 8)
                    VBc = VB[c]
                    kVB = ("VB", c)
                    S.add("act", ACT(SG[:, :n], ps[bgi][:, :n], AF.Sigmoid), reads=[("ps", bgi)], writes=[kSG])
                    if c0 == 0:
                        S.add("dve", TT(SG[:, :HALO], ps[bvi][:, :HALO], SG[:, :HALO], ALU.mult), reads=[("ps", bvi), kSG], writes=[kSG])
                        S.add("dve", TS1(VBc[:, 0:HALO], SG[:, :HALO], V("mhalo"), ALU.mult), reads=[kSG, "vecs"], writes=[kVB])
                        continue
                    S.add("dve", TT(VBc[:, HALO:HALO + 512], ps[bvi][:, :], SG, ALU.mult), reads=[("ps", bvi), kSG], writes=[kVB])
                    S.add("dve", TS(CV, VBc[:, HALO:HALO + 512], Vc(("bcw", 30), c), Vc("bcb", c), ALU.mult, ALU.add),
                          reads=[kVB, "vecs"], writes=[kCV])
                    for k in range(0, 18):
                        S.add("dve", STT(CV, VBc[:, HALO - 30 + k:HALO - 30 + k + 512], Vc(("bcw", k), c), CV, ALU.mult, ALU.add),
                              reads=[kVB, "vecs", kCV], writes=[kCV])
                    S.add("act", ACT(CV2, VBc[:, HALO - 30 + 18:HALO - 30 + 18 + 512], AF.Identity, scale=Vc(("bcw", 18), c)),
                          reads=[kVB, "vecs"], writes=[kCV2])
                    for k in range(19, 30):
                        PB, kPB = blk(5 + (k % 2) + jj * 8)
                        S.add("act", ACT(PB, VBc[:, HALO - 30 + k:HALO - 30 + k + 512], AF.Identity, scale=Vc(("bcw", k), c)),
                              reads=[kVB, "vecs"], writes=[kPB])
                        S.add("pool", TT(CV2, CV2, PB, ALU.add), reads=[kCV2, kPB], writes=[kCV2])
                    S.add("pool", TT(CV, CV, CV2, ALU.add), reads=[kCV, kCV2], writes=[kCV])
                    S.add("pool", CP(VBc[:, 0:HALO], VBc[:, 512:512 + HALO]), reads=[kVB], writes=[kVB])
                    q1 = rotate("sqb", sqb)
                    S.add("act", ACT(sqb[q1][:], CV, AF.Copy), reads=[kCV], writes=[("sqb", q1)])
                    mi_ = nextps()
                    S.add("pe", MM(ps[mi_][:, :], bd64_b[:], sqb[q1][:]), reads=[("sqb", q1), "bd64_b"], writes=[("ps", mi_)])
                    q2 = rotate("sqb", sqb)
                    S.add("act", ACT(sqb[q2][:], CV, AF.Square), reads=[kCV], writes=[("sqb", q2)])
                    vi_ = nextps()
                    S.add("pe", MM(ps[vi_][:, :], bd64_b[:], sqb[q2][:]), reads=[("sqb", q2), "bd64_b"], writes=[("ps", vi_)])
                    S.add("act", ACT(MN, ps[mi_][:, :], AF.Copy, scale=1.0 / 64), reads=[("ps", mi_)], writes=[kMN])
                    S.add("dve", TT(VR, MN, MN, ALU.mult), reads=[kMN], writes=[kVR])
                    S.add("dve", STT(VR, ps[vi_][:, :], 1.0 / 64, VR, ALU.mult, ALU.subtract), reads=[("ps", vi_), kVR], writes=[kVR])
                    S.add("act", ACT(VR, VR, AF.Sqrt, bias=EPSAP, scale=1.0), reads=[kVR, "cst"], writes=[kVR])
                    S.add("dve", RCP(VR, VR), reads=[kVR], writes=[kVR])
                    S.add("dve", TT(CV, CV, MN, ALU.subtract), reads=[kCV, kMN], writes=[kCV])
                    S.add("dve", TT(CV, CV, VR, ALU.mult), reads=[kCV, kVR], writes=[kCV])
                    if mode == "direct":
                        S.add("act", ACT(act[:, 4 + c, c0:c0 + n], CV, AF.Silu, bias=Vc("bnb", c), scale=Vc("bng", c)),
                              reads=[kCV, "vecs"], writes=[("act", 4 + c, jkey(c0))])
                        continue
                    S.add("act", ACT(CV, CV, AF.Silu, bias=Vc("bnb", c), scale=Vc("bng", c)), reads=[kCV, "vecs"], writes=[kCV])
                    d0 = tok0 + c0 - HALO
                    S.dma("sp", DMA(yb_d[:, c, d0:d0 + 512], CV), reads=[kCV], writes=[("dY", c, t)], rr=("st", 8))

        def mixer_c(subtiles, pre=True, post=True):
            if pre:
                prenorm(1, 1, subtiles)
            psrot["pool"] = [0, 1, 2, 3]
            vf = act[:, 0:16, :].rearrange("p a b -> p (a b)")[:, 0:16384].bitcast(F32).rearrange("p (a b) -> p a b", a=16)
            actkeys = [("act", fc, j) for fc in range(FC) for j in range(3)]
            vkeys = [("vf", i) for i in range(16)]
            S.alias(actkeys, vkeys)
            for c in range(8):
                si = load_w13(cin_d[c])
                for (c0, n) in subtiles:
                    jj = jkey(c0) - 1
                    ui = proj8(si, 0, c0, n)
                    vi = proj8(si, 1, c0, n)
                    U, kU = blk(c * 2 + jj)
                    Vv, kV = vf[:, c * 2 + jj, :], ("vf", c * 2 + jj)
                    S.add("act", ACT(U, ps[ui][:, :], AF.Gelu_apprx_tanh, bias=Vc("cbu", c)), reads=[("ps", ui), "vecs"], writes=[kU])
                    S.add("act", ACT(Vv, ps[vi][:, :], AF.Gelu_apprx_tanh, bias=Vc("cbv", c)), reads=[("ps", vi), "vecs"], writes=[kV])
                    q1 = rotate("sqb", sqb)
                    S.add("act", ACT(sqb[q1][:], Vv, AF.Copy), reads=[kV], writes=[("sqb", q1)])
                    S.add("pe", MM(ps[4 + jj][:, :], ones_b[:], sqb[q1][:], start=(c == 0), stop=(c == 7)),
                          reads=[("sqb", q1), "ones_b"], writes=[("ps", 4 + jj)])
                    q2 = rotate("sqb", sqb)
                    S.add("dve", TT(sqb[q2][:], Vv, Vv, ALU.mult), reads=[kV], writes=[("sqb", q2)])
                    S.add("pe", MM(ps[6 + jj][:, :], ones_b[:], sqb[q2][:], start=(c == 0), stop=(c == 7)),
                          reads=[("sqb", q2), "ones_b"], writes=[("ps", 6 + jj)])
            for (c0, n) in subtiles:
                j = jkey(c0)
                jj = j - 1
                MN = rstd[:, c0:c0 + n]
                RS = rstd2[:, c0:c0 + n]
                kMN, kRS = ("rstd", j), ("rstd2", j)
                S.add("act", ACT(MN, ps[4 + jj][:, :], AF.Copy, scale=1.0 / D), reads=[("ps", 4 + jj)], writes=[kMN])
                S.add("dve", TT(RS, MN, MN, ALU.mult), reads=[kMN], writes=[kRS])
                S.add("dve", STT(RS, ps[6 + jj][:, :], 1.0 / D, RS, ALU.mult, ALU.subtract), reads=[("ps", 6 + jj), kRS], writes=[kRS])
                S.add("act", ACT(RS, RS, AF.Sqrt, bias=EPSAP, scale=1.0), reads=[kRS, "cst"], writes=[kRS])
                S.add("dve", RCP(RS, RS), reads=[kRS], writes=[kRS])
                for c in range(8):
                    Vv, kV = vf[:, c * 2 + jj, :], ("vf", c * 2 + jj)
                    U, kU = blk(c * 2 + jj)
                    khb = ("hb", c, j)
                    S.add("dve", TT(Vv, Vv, MN, ALU.subtract), reads=[kV, kMN], writes=[kV])
                    S.add("dve", TT(Vv, Vv, RS, ALU.mult), reads=[kV, kRS], writes=[kV])
                    S.add("act", ACT(hb[:, c, c0:c0 + n], Vv, AF.Identity, bias=Vc("cnb", c), scale=Vc("cng", c)),
                          reads=[kV, "vecs"], writes=[khb])
                    ti = nextps()
                    pT = ps[ti][:, :].bitcast(BF16)
                    S.add("pe", [TR(pT[:, b * 128:(b + 1) * 128], hb[:, c, c0 + b * 128:c0 + (b + 1) * 128], ident_b[:]) for b in range(4)],
                          reads=[khb, "ident_b"], writes=[("ps", ti)])
                    vi_ = rotate("vTb", vTb)
                    S.add("act", ACT(vTb[vi_][:], pT[:, 0:512], AF.Copy), reads=[("ps", ti)], writes=[("vTb", vi_)])
                    mi_ = nextps()
                    S.add("pe", [MM(ps[mi_][:, b * 128:(b + 1) * 128], vTb[vi_][:, b * 128:(b + 1) * 128], wsT_b[:, c, :]) for b in range(4)],
                          reads=[("vTb", vi_), "wsT_b"], writes=[("ps", mi_)])
                    tf = rotate("tmpf", tmpf)
                    for b in range(4):
                        S.add("dve", TT(tmpf[tf][:, b * 128:(b + 1) * 128], ps[mi_][:, b * 128:(b + 1) * 128], bsb[:, c, :], ALU.add),
                              reads=[("ps", mi_), "bsb"], writes=[("tmpf", tf)])
                    S.add("dve", TT(hb[:, c, c0:c0 + n], tmpf[tf][:], U, ALU.mult), reads=[("tmpf", tf), kU], writes=[khb])
            S.alias(vkeys, actkeys)
            psrot["pool"] = [0, 1, 2, 3, 4]
            out_proj(cout_d, 1, subtiles, post=post)

        main_st = [(HALO, 512), (HALO + 512, 512)]
        if redun:
            xkeys_main = [("xres", c, j) for c in range(8) for j in (1, 2)]
            xkeys_all = [("xres", c, j) for c in range(8) for j in range(3)]
            for pt in range(3 * nt):
                S.dma("sp", DMA(xres[:, :, HALO:WCOL], xpre_d[:, :, pt * TILE:(pt + 1) * TILE]), writes=xkeys_main, semkey="xload")
                ffn(0, 0, main_st, post=False)
                trans(0, 0, 0, 1, main_st)
                mixer_ab_in(pt, main_st, mode="prefix", pre=False)
            for t in range(nt):
                subt = ([(0, HALO)] if t == 0 else []) + main_st
                lo = 0 if t == 0 else HALO
                src0 = t * TILE + lo
                S.dma("sp", DMA(xres[:, :, lo:WCOL], xin_d[:, :, src0:(t + 1) * TILE + HALO]), writes=xkeys_all, semkey="xload")
                if nsub <= 1:
                    ffn(0, 0, subt)
                else:
                    ffn(0, 0, subt, post=False)
                    trans(0, 0, 0, 1, subt)
                    mixer_ab_in(t, subt, mode="direct", pre=False)
                    out_proj(about_d, 0, main_st, src=act, srckey="act", post=(nsub == 2))
                if nsub > 2:
                    trans(0, 1, 0, 2, main_st)
                    ffn(0, 1, main_st, pre=False, post=(nsub == 3))
                if nsub > 3:
                    trans(0, 2, 1, 0, main_st)
                    ffn(1, 0, main_st, pre=False, post=(nsub == 4))
                if nsub > 4:
                    trans(1, 0, 1, 1, main_st)
                    mixer_c(main_st, pre=False, post=(nsub == 5))
                if nsub > 5:
                    trans(1, 1, 1, 2, main_st)
                    ffn(1, 1, main_st, pre=False, post=True)
                S.dma("sp", DMA(out_d[:, :, t * TILE:(t + 1) * TILE], xres[:, :, HALO:WCOL]), reads=xkeys_main, writes=[("dO", t)], rr=("st", 8))
            S.final_wait("sp", [("dO", t) for t in range(nt)])
        if doA and not redun:
            for t in range(nt):
                subt = ([(0, HALO)] if t == 0 else []) + main_st
                lo = 0 if t == 0 else HALO
                src0 = t * TILE + lo
                keys = [("xres", c, j) for c in range(8) for j in range(3)]
                S.dma("sp", DMA(xres[:, :, lo:WCOL], xin_d[:, :, src0:(t + 1) * TILE + HALO]), writes=keys, semkey="xload")
                ffn(0, 0, subt)
                S.dma("sp", DMA(x1_d[:, :, t * TILE:(t + 1) * TILE], xres[:, :, HALO:WCOL]),
                      reads=[("xres", c, j) for c in range(8) for j in (1, 2)], writes=[("dX", t)], rr=("st", 8))
                mixer_ab_in(t, subt)
            S.dma("sp", DMA(st_d[:, 0:4], acar[:]), reads=["acar"], writes=["dst"], rr=("st", 8))
            S.dma("sp", DMA(st_d[:, 4:8], hcar[:]), reads=["hcar"], writes=["dst2"], rr=("st", 8))
            if not fused:
                S.final_wait("sp", ["dst", "dst2"] + [("dX", t) for t in range(nt)] +
                             [(nm, c, t) for nm in ("dP", "dQ", "dY") for c in range(4) for t in range(nt)])
        if fused:
            S.add("pool", lambda e: e.collective_compute("AllGather", ALU.bypass, replica_groups=[list(range(NCORES))],
                                                        ins=[st_d[:, :]], outs=[stall_d[:, :]]),
                  reads=["dst", "dst2"], writes=["dstall"], semkey="cc", inc=16)
        if doB and not redun:
            S.dma("sp", DMA(stall[:], stall_d.rearrange("(r p) c -> p r c", p=128)), reads=["dstall"], writes=["stall"], rr=("ld", 8))
            S.add("dve", MSET(carry[:], 0.0), writes=["carry"])
            for s in range(NCORES):
                S.add("dve", TT(ctmp[0][:], stall[:, s, 0:4], carry[:], ALU.mult), reads=["stall", "carry"], writes=["ctmp0"])
                S.add("dve", TT(ctmp[0][:], ctmp[0][:], stall[:, s, 4:8], ALU.add), reads=["stall", "ctmp0"], writes=["ctmp0"])
                S.add("dve", TT(ctmp[0][:], ctmp[0][:], carry[:], ALU.subtract), reads=["carry", "ctmp0"], writes=["ctmp0"])
                S.add("dve", STT(carry[:], ctmp[0][:], Vc("mprev", s), carry[:], ALU.mult, ALU.add),
                      reads=["ctmp0", "carry", "vecs"], writes=["carry"])
            for t in range(nt):
                keys = [("xres", c, j) for c in range(8) for j in (1, 2)]
                S.dma("sp", DMA(xres[:, :, HALO:WCOL], x1_d[:, :, t * TILE:(t + 1) * TILE]), reads=[("dX", t)], writes=keys, semkey="xload")
                if nsub > 1:
                    for c in range(4):
                        S.dma("sp", DMA(ybuf[:, c * 2:c * 2 + 2, :], P_d[:, c, t * TILE:(t + 1) * TILE].rearrange("p (a b) -> p a b", a=2)),
                              reads=[("dP", c, t)], writes=[("yb", c * 2), ("yb", c * 2 + 1)], rr=("ld", 8))
                        S.dma("sp", DMA(ybuf[:, 8 + c * 2:8 + c * 2 + 2, :], Q_d[:, c, t * TILE:(t + 1) * TILE].rearrange("p (a b) -> p a b", a=2)),
                              reads=[("dQ", c, t)], writes=[("yb", 8 + c * 2), ("yb", 8 + c * 2 + 1)], rr=("ld", 8))
                        S.dma("pool", DMA(hb[:, 4 + c, HALO:WCOL], yb_d[:, c, t * TILE:(t + 1) * TILE]),
                              reads=[("dY", c, t)], writes=[("hb", 4 + c, 1), ("hb", 4 + c, 2)], rr=("lq", 4))
                        for jj in range(2):
                            S.add("dve", STT(hb[:, c, HALO + jj * 512:HALO + (jj + 1) * 512], ybuf[:, 8 + c * 2 + jj, :], carry[:, c:c + 1],
                                             ybuf[:, c * 2 + jj, :], ALU.mult, ALU.add),
                                  reads=[("yb", c * 2 + jj), ("yb", 8 + c * 2 + jj), "carry"], writes=[("hb", c, jj + 1)])
                    out_proj(about_d, 0, main_st)
                if nsub > 2:
                    ffn(0, 1, main_st)
                if nsub > 3:
                    ffn(1, 0, main_st)
                if nsub > 4:
                    mixer_c(main_st)
                if nsub > 5:
                    ffn(1, 1, main_st)
                S.dma("sp", DMA(out_d[:, :, t * TILE:(t + 1) * TILE], xres[:, :, HALO:WCOL]), reads=keys, writes=[("dO", t)], rr=("st", 8))
            S.final_wait("sp", [("dO", t) for t in range(nt)])

        global _LAST_SCHED
        _LAST_SCHED = S
        if os.environ.get("KERNEL_PLAN_ONLY"):
            return None
        with nc.Block() as block:
            def run_stream(eng, items):
                for waits, fns, incinfo in items:
                    for sk, v in waits:
                        eng.wait_ge(sems[sk], v)
                    ins = None
                    for fn in fns:
                        ins = fn(eng)
                    if incinfo is not None:
                        ins.then_inc(sems[incinfo[0]], incinfo[1])

            @block.tensor
            def _(e):
                run_stream(e, S.streams["pe"])

            @block.scalar
            def _(e):
                run_stream(e, S.streams["act"])

            @block.vector
            def _(e):
                run_stream(e, S.streams["dve"])

            @block.gpsimd
            def _(e):
                run_stream(e, S.streams["pool"])

            @block.sync
            def _(e):
                run_stream(e, S.streams["sp"])
    return nc


def _fm(v, n):
    return np.ascontiguousarray(np.asarray(v, np.float32).reshape(n, 128).T)


def _slotK(w, cols_list):
    K = w.shape[0]
    sel = np.concatenate([w[:, a:b] for a, b in cols_list], axis=1)
    return np.ascontiguousarray(sel.reshape(K // 128, 128, -1).transpose(1, 0, 2).reshape(128, -1))


def prep_shared(inp):
    f = lambda a: np.asarray(a, np.float32)
    sh = {}
    ada_w = f(inp["ada_w"])
    sh["adaw"] = np.ascontiguousarray(
        ada_w.reshape(2, 8, 128, 18, 512).transpose(0, 3, 2, 1, 4).reshape(36, 128, 4096))
    sh["adab"] = np.ascontiguousarray(f(inp["ada_b"]).reshape(36, 512))
    w13 = f(inp["ffn_w13"])
    w2 = f(inp["ffn_w2"])
    w13L = np.empty((88, 128, 2048), np.float32)
    w2L = np.empty((32, 128, 2816), np.float32)
    for l in range(2):
        for ff in range(2):
            for fc in range(FC):
                w13L[(l * 2 + ff) * 22 + fc] = _slotK(w13[l, ff], [(fc * 128, fc * 128 + 128), (DFF + fc * 128, DFF + fc * 128 + 128)])
            for dc in range(8):
                w2L[(l * 2 + ff) * 8 + dc] = _slotK(w2[l, ff], [(dc * 128, dc * 128 + 128)])
    sh["w13L"] = w13L
    sh["w2L"] = w2L
    wi = f(inp["ab_w_in"])[0]
    sh["abinL"] = np.stack(
        [_slotK(wi, [(512 + c * 128, 512 + c * 128 + 128), (c * 128, c * 128 + 128)]) for c in range(4)] +
        [_slotK(wi, [(1024 + c * 128, 1024 + c * 128 + 128), (1536 + c * 128, 1536 + c * 128 + 128)]) for c in range(4)])
    wo = f(inp["ab_w_out"])[0]
    sh["aboutL"] = np.stack([_slotK(wo, [(dc * 128, dc * 128 + 128)]) for dc in range(8)])
    ci = f(inp["c_w_in"])[0]
    sh["cinL"] = np.stack([_slotK(ci, [(c * 128, c * 128 + 128), (1024 + c * 128, 1024 + c * 128 + 128)]) for c in range(8)])
    co = f(inp["c_w_out"])[0]
    sh["coutL"] = np.stack([_slotK(co, [(dc * 128, dc * 128 + 128)]) for dc in range(8)])
    gw = f(inp["a_gate_w"])[0]
    gbd = np.zeros((128, 8, 128), np.float32)
    for c in range(4):
        for h2 in range(2):
            gbd[h2 * 64:(h2 + 1) * 64, c * 2 + 0, h2 * 64:(h2 + 1) * 64] = gw[2 * c + h2][:, 0:64]
            gbd[h2 * 64:(h2 + 1) * 64, c * 2 + 1, h2 * 64:(h2 + 1) * 64] = gw[2 * c + h2][:, 64:128]
    sh["gatebd"] = gbd.reshape(128, 1024)
    ws = f(inp["c_w_s"])[0]
    sh["wsT"] = np.ascontiguousarray(ws.transpose(2, 0, 1).reshape(128, 1024))
    sh["bs"] = np.ascontiguousarray(f(inp["c_b_s"])[0].reshape(1, 1024))
    cst = np.zeros((128, 384), np.float32)
    cst[:, 0:128] = np.eye(128, dtype=np.float32)
    cst[:, 128:256] = np.triu(np.ones((128, 128), np.float32))
    cst[0:64, 256:320] = 1.0
    cst[64:128, 320:384] = 1.0
    sh["consts"] = cst
    return sh


def prep_vecs(inp, core, ntok=4096):
    f = lambda a: np.asarray(a, np.float32)
    b = core // 4
    pos = core % 4
    v = np.zeros((128, NV), np.float32)

    def put(name, arr):
        v[:, VOFF[name]:VOFF[name] + arr.shape[1]] = arr

    put("c", _fm(f(inp["c"])[b], 8))
    for l in range(2):
        for s in range(3):
            put(("npre", l, s), _fm(f(inp["norm_pre"])[l, s], 8))
            put(("npost", l, s), _fm(f(inp["norm_post"])[l, s], 8))
    for k in range(4):
        put(("acw", k), _fm(f(inp["a_conv_w"])[0, k], 4))
    put("acb", _fm(f(inp["a_conv_b"])[0], 4))
    gb = f(inp["a_gate_b"])[0]
    put("gbr", _fm(gb[:, 0:64].reshape(512), 4))
    put("gbi", _fm(gb[:, 64:128].reshape(512), 4))
    put("lam", _fm(f(inp["a_lam"])[0], 4))
    for k in range(31):
        put(("bcw", k), _fm(f(inp["b_conv_w"])[0, k], 4))
    put("bcb", _fm(f(inp["b_conv_b"])[0], 4))
    put("bng", _fm(f(inp["b_norm_g"])[0], 4))
    put("bnb", _fm(f(inp["b_norm_b"])[0], 4))
    cb = f(inp["c_b_in"])[0]
    put("cbu", _fm(cb[0:1024], 8))
    put("cbv", _fm(cb[1024:2048], 8))
    put("cng", _fm(f(inp["c_norm_g"])[0], 8))
    put("cnb", _fm(f(inp["c_norm_b"])[0], 8))
    v[:, VOFF["mhalo"]] = 1.0 if pos > 0 else 0.0
    for s in range(NCORES):
        v[:, VOFF["mprev"] + s] = 1.0 if (s // 4 == b and s % 4 < pos) else 0.0
    nt = ntok // TILE
    for pt in range(min(3 * nt, 12)):
        v[:, VOFF["mpre"] + pt] = 1.0 if pt >= (3 - pos) * nt else 0.0
    return v


def prep_x(x, core, ntok):
    b = core // 4
    pos = core % 4
    t0 = pos * ntok
    xs = np.zeros((ntok + HALO, D), np.float32)
    xs[HALO:] = x[b, t0:t0 + ntok]
    if pos > 0:
        xs[:HALO] = x[b, t0 - HALO:t0]
    return np.ascontiguousarray(xs.T.reshape(8, 128, ntok + HALO).transpose(1, 0, 2))


def prep_xpre(x, core, ntok):
    b = core // 4
    pos = core % 4
    t0 = pos * ntok
    xs = np.zeros((3 * ntok, D), np.float32)
    if pos > 0:
        xs[3 * ntok - t0:] = x[b, 0:t0]
    return np.ascontiguousarray(xs.T.reshape(8, 128, 3 * ntok).transpose(1, 0, 2))


_NC_CACHE = {}


def _get_nc(ntok, phase, nsub=6):
    key = (ntok, phase, nsub)
    if key not in _NC_CACHE:
        _NC_CACHE[key] = build(ntok, phase, nsub)
    return _NC_CACHE[key]


def run_all(inputs, nsub=6, fused=False):
    x = np.asarray(inputs["x"], np.float32)
    B, SEQ, _ = x.shape
    ntok = SEQ // 4
    sh = prep_shared(inputs)
    vec = [prep_vecs(inputs, k, ntok) for k in range(NCORES)]
    cores = list(range(NCORES))
    keysA = ["consts", "adaw", "adab", "w13L", "w2L", "abinL", "gatebd"]
    keysB = ["consts", "adaw", "adab", "w13L", "w2L", "aboutL", "cinL", "coutL", "wsT", "bs"]
    if fused:
        nc = _get_nc(ntok, "F", nsub)
        maps = []
        for k in cores:
            m = {kk: sh[kk] for kk in set(keysA + keysB)}
            m["vecs"] = vec[k]
            m["xin"] = prep_x(x, k, ntok)
            m["xpre"] = prep_xpre(x, k, ntok)
            maps.append(m)
        res = run_bass_kernel_spmd(nc, maps, core_ids=cores)
        outs = [r["out"] for r in res.results]
    else:
        ncA = _get_nc(ntok, "A")
        maps = []
        for k in cores:
            m = {kk: sh[kk] for kk in keysA}
            m["vecs"] = vec[k]
            m["xin"] = prep_x(x, k, ntok)
            maps.append(m)
        resA = run_bass_kernel_spmd(ncA, maps, core_ids=cores)
        stall = np.ascontiguousarray(np.concatenate([r["st"] for r in resA.results], axis=0))
        ncB = _get_nc(ntok, "B", nsub)
        maps = []
        for k in cores:
            m = {kk: sh[kk] for kk in keysB}
            m["vecs"] = vec[k]
            r = resA.results[k]
            for kk in ("x1T", "PT", "QT", "ybT"):
                m[kk] = r[kk]
            m["stall"] = stall
            maps.append(m)
        resB = run_bass_kernel_spmd(ncB, maps, core_ids=cores)
        outs = [r["out"] for r in resB.results]
    out = np.empty((B, SEQ, D), np.float32)
    for k in cores:
        b, pos = k // 4, k % 4
        o = outs[k]
        out[b, pos * ntok:(pos + 1) * ntok] = o.transpose(2, 1, 0).reshape(ntok, D)
    return out


def kernel(**inputs):
    return run_all(inputs, nsub=6, fused=True)
```

```python
import os
import numpy as np
import concourse.bass as bass
import concourse.mybir as mybir
from concourse.bass_utils import run_bass_kernel_spmd

F32 = mybir.dt.float32
BF16 = mybir.dt.bfloat16
AF = mybir.ActivationFunctionType
ALU = mybir.AluOpType

D = 1024
DC = 8
DFF = 2816
FC = 22
WA = 512
TILE = 1024
HALO = 32
WCOL = HALO + TILE
EPS = 1e-6
NCORES = 8
PUMP_N = 5

VOFF = {}


def _mk_voff():
    o = 0

    def add(name, n):
        nonlocal o
        VOFF[name] = o
        o += n

    add("c", 8)
    for l in range(2):
        for s in range(3):
            add(("npre", l, s), 8)
    for l in range(2):
        for s in range(3):
            add(("npost", l, s), 8)
    for k in range(4):
        add(("acw", k), 4)
    add("acb", 4)
    add("gbr", 4)
    add("gbi", 4)
    add("lam", 4)
    for k in range(31):
        add(("bcw", k), 4)
    add("bcb", 4)
    add("bng", 4)
    add("bnb", 4)
    add("cbu", 8)
    add("cbv", 8)
    add("cng", 8)
    add("cnb", 8)
    add("mhalo", 1)
    add("mprev", 8)
    add("mpre", 12)
    return o


NV = _mk_voff()


class Sched:
    def __init__(self):
        self.streams = {k: [] for k in ("pe", "act", "dve", "pool", "sp")}
        self.semval = {}
        self.lastw = {}
        self.readers = {}
        self.seen = {k: {} for k in self.streams}
        self.rr = {}

    def _deps(self, reads, writes):
        deps = set()
        for r in reads:
            t = self.lastw.get(r)
            if t:
                deps.add(t)
        for w in writes:
            t = self.lastw.get(w)
            if t:
                deps.add(t)
            for t in self.readers.get(w, ()):
                deps.add(t)
        return deps

    def add(self, stream, fns, reads=(), writes=(), semkey=None, inc=1, extra=()):
        deps = self._deps(reads, writes)
        deps.update(extra)
        semkey = semkey or stream
        self.semval[semkey] = self.semval.get(semkey, 0) + inc
        tk = (semkey, self.semval[semkey])
        best = {}
        for sk, v in deps:
            if stream == "pe" and sk == "pe":
                continue
            if v > best.get(sk, 0):
                best[sk] = v
        waits = []
        for sk, v in best.items():
            if self.seen[stream].get(sk, 0) >= v:
                continue
            self.seen[stream][sk] = v
            waits.append((sk, v))
        if not isinstance(fns, (list, tuple)):
            fns = [fns]
        self.streams[stream].append((waits, list(fns), (semkey, inc)))
        for w in writes:
            self.lastw[w] = tk
            self.readers[w] = []
        for r in reads:
            self.readers.setdefault(r, []).append(tk)
        return tk

    def dma(self, stream, fn, reads=(), writes=(), semkey=None, rr=None):
        extra = ()
        if rr is not None:
            cls, R = rr
            i = self.rr.get(cls, 0)
            self.rr[cls] = i + 1
            semkey = "%s%d" % (cls, i % R)
            prev = self.lastw.get(("__sem", semkey))
            if prev:
                extra = (prev,)
        tk = self.add(stream, fn, reads, writes, semkey=semkey, inc=16, extra=extra)
        if rr is not None:
            self.lastw[("__sem", semkey)] = tk
        return tk

    def alias(self, old_keys, new_keys):
        ts = set()
        for k in old_keys:
            t = self.lastw.get(k)
            if t:
                ts.add(t)
            for t in self.readers.get(k, ()):
                ts.add(t)
        for k in new_keys:
            self.readers.setdefault(k, []).extend(ts)

    def final_wait(self, stream, keys):
        deps = self._deps(keys, ())
        best = {}
        for sk, v in deps:
            if v > best.get(sk, 0):
                best[sk] = v
        self.streams[stream].append((list(best.items()), [], None))


def MM(out, lhsT, rhs, start=True, stop=True):
    return lambda e: e.matmul(out, lhsT, rhs, start=start, stop=stop)


def TR(out, in_, ident):
    return lambda e: e.transpose(out, in_, ident)


def ACT(out, in_, func, bias=None, scale=None):
    kw = {}
    if bias is not None:
        kw["bias"] = bias
    if scale is not None:
        kw["scale"] = scale
    return lambda e: e.activation(out=out, in_=in_, func=func, **kw)


def TT(out, in0, in1, op):
    return lambda e: e.tensor_tensor(out=out, in0=in0, in1=in1, op=op)


def TS(out, in0, s1, s2, op0, op1):
    return lambda e: e.tensor_scalar(out=out, in0=in0, scalar1=s1, scalar2=s2, op0=op0, op1=op1)


def TS1(out, in0, s1, op):
    return lambda e: e.tensor_single_scalar(out=out, in_=in0, scalar=s1, op=op)


def STT(out, in0, scalar, in1, op0, op1):
    return lambda e: e.scalar_tensor_tensor(out=out, in0=in0, scalar=scalar, in1=in1, op0=op0, op1=op1)


def SCAN(out, d0, d1, init):
    return lambda e: e.tensor_tensor_scan(out=out, data0=d0, data1=d1, initial=init, op0=ALU.mult, op1=ALU.add)


def CP(out, in_):
    return lambda e: e.tensor_copy(out=out, in_=in_)


def RCP(out, in_):
    return lambda e: e.reciprocal(out=out, in_=in_)


def MSET(ap, v):
    return lambda e: e.memset(ap, v)


def DMA(out, in_):
    return lambda e: e.dma_start(out=out, in_=in_)


def build(ntok, phase, nsub=6):
    nt = ntok // TILE
    nc = bass.Bass("TRN2", target_bir_lowering=False)
    S = Sched()
    redun = phase == "F"
    doA = "A" in phase or redun
    doB = "B" in phase or redun
    fused = phase == "AB"

    def din(name, shape, dt=F32):
        return nc.dram_tensor(name, shape, dt, kind="ExternalInput").ap()

    def dout(name, shape, dt=F32):
        return nc.dram_tensor(name, shape, dt, kind="ExternalOutput").ap()

    def dint(name, shape, dt=F32):
        return nc.dram_tensor(name, shape, dt, kind="Internal").ap()

    vecs_d = din("vecs", [128, NV])
    consts_d = din("consts", [128, 384])
    adaw_d = din("adaw", [36, 128, 4096])
    adab_d = din("adab", [36, 512])
    w13_d = din("w13L", [88, 128, 2048])
    w2_d = din("w2L", [32, 128, 2816])
    if doA:
        xin_d = din("xin", [128, 8, ntok + HALO])
        abin_d = din("abinL", [8, 128, 2048])
        gatebd_d = din("gatebd", [128, 1024])
    if doB:
        about_d = din("aboutL", [8, 128, 1024])
        cin_d = din("cinL", [8, 128, 2048])
        cout_d = din("coutL", [8, 128, 1024])
        wsT_d = din("wsT", [128, 1024])
        bs_d = din("bs", [1, 1024])
        out_d = dout("out", [128, 8, ntok])
    if redun:
        xpre_d = din("xpre", [128, 8, 3 * ntok])
    elif fused:
        x1_d = dint("x1T", [128, 8, ntok])
        P_d = dint("PT", [128, 4, ntok])
        Q_d = dint("QT", [128, 4, ntok])
        yb_d = dint("ybT", [128, 4, ntok])
        st_d = dint("st", [128, 8])
        stall_d = dint("stall", [NCORES * 128, 8])
    else:
        mk = dout if doA else din
        x1_d = mk("x1T", [128, 8, ntok])
        P_d = mk("PT", [128, 4, ntok])
        Q_d = mk("QT", [128, 4, ntok])
        yb_d = mk("ybT", [128, 4, ntok])
        if doA:
            st_d = dout("st", [128, 8])
        else:
            stall_d = din("stall", [NCORES * 128, 8])

    import contextlib
    es = contextlib.ExitStack()
    with es:
        def sb(name, shape, dt=F32):
            return es.enter_context(nc.sbuf_tensor("s_" + name, shape, dt))

        xres = sb("xres", [128, 8, WCOL])
        hb = sb("hb", [128, 8, WCOL], BF16)
        act = sb("act", [128, FC, WCOL], BF16)
        ybuf = sb("ybuf", [128, 16, 512])
        yhalo = sb("yhalo", [128, 8, HALO])
        w13s = [sb("w13s%d" % i, [128, 8, 256], BF16) for i in range(2)]
        w2s = [sb("w2s%d" % i, [128, FC, 128], BF16) for i in range(2)]
        sqb = [sb("sqb%d" % i, [128, 512], BF16) for i in range(4)]
        rstd = sb("rstd", [128, WCOL])
        rstd2 = sb("rstd2", [128, WCOL])
        tmpf = [sb("tmpf%d" % i, [128, 512]) for i in range(3)]
        vecs = sb("vecs", [128, NV])
        modT = sb("modT", [128, 144])
        gsT = sb("gsT", [128, 48])
        coefT = sb("coefT", [128, 48])
        cst = sb("cst", [128, 4])
        ones_b = sb("ones_b", [128, 128], BF16)
        one_f = sb("one_f", [1, 1])
        cact = sb("cact", [128, 8])
        cact_b = sb("cact_b", [128, 8], BF16)
        if doA:
            AXH = 4
            AX = [sb("AX%d" % i, [128, AXH + TILE] if redun else [128, HALO + 512]) for i in range(4)]
            VB = [sb("VB%d" % i, [128, HALO + 512]) for i in range(4)] if not redun else None
            vhalo = sb("vhalo", [128, 4, HALO], BF16) if redun else None
            gate_b = sb("gate_b", [128, 8, 128], BF16)
            bd64_b = sb("bd64_b", [128, 128], BF16)
            zeros = sb("zeros", [128, 512]) if not redun else None
            hcar = sb("hcar", [128, 4])
            acar = sb("acar", [128, 4])
            c8 = sb("c8", [128, 4])
            c16 = sb("c16", [128, 4])
            lt = [sb("lt%d" % i, [128, 4]) for i in range(2)]
            xrb = [sb("xrb%d" % i, [128, 512], BF16) for i in range(4)]
        if doB:
            ident_b = sb("ident_b", [128, 128], BF16)
            tril = sb("tril", [128, 128])
            wsT_b = sb("wsT_b", [128, 8, 128], BF16)
            bsb = sb("bsb", [128, 8, 128])
            if not redun:
                stall = sb("stall_sb", [128, NCORES, 8])
                carry = sb("carry", [128, 4])
                ctmp = [sb("ctmp%d" % i, [128, 4]) for i in range(2)]
            vTb = [sb("vTb%d" % i, [128, 512], BF16) for i in range(2)]
        ps = [es.enter_context(nc.psum_tensor("ps%d" % i, [128, 512], F32)) for i in range(8)]
        if os.environ.get("KERNEL_PLAN_ONLY"):
            print("SBUF bytes/partition remaining:", nc.sbuf_bytes_remaining)
        sem_names = ["pe", "act", "dve", "pool", "xload", "w13s0", "w13s1", "w2s0", "w2s1",
                     "misc", "cc"] + ["st%d" % i for i in range(8)] + ["ld%d" % i for i in range(8)] + ["lq%d" % i for i in range(4)] + ["ada0", "ada1"]
        sems = {n: es.enter_context(nc.semaphore(n)) for n in sem_names}

        psrot = {"pool": [0, 1, 2, 3, 4], "i": 0}

        def nextps():
            i = psrot["pool"][psrot["i"] % len(psrot["pool"])]
            psrot["i"] += 1
            return i

        V = lambda name, n=1: vecs[:, VOFF[name]:VOFF[name] + n]
        Vc = lambda name, c: vecs[:, VOFF[name] + c:VOFF[name] + c + 1]
        EPSAP = cst[:, 0:1]
        ONEAP = cst[:, 1:2]

        rot = {}

        def rotate(name, lst):
            i = rot.get(name, 0)
            rot[name] = i + 1
            return i % len(lst)

        S.dma("sp", DMA(vecs[:], vecs_d[:, :]), writes=["vecs"], semkey="misc")
        S.add("dve", MSET(cst[:, 0:1], EPS), writes=["cst"])
        S.add("dve", MSET(cst[:, 1:2], 1.0), writes=["cst"])
        S.add("dve", MSET(ones_b[:], 1.0), writes=["ones_b"])
        S.add("dve", MSET(one_f[:], 1.0), writes=["one_f"])
        if doA:
            if redun:
                for c_ in range(4):
                    S.add("dve", MSET(AX[c_][:, 0:AXH], 0.0), writes=[("AX", c_, "h")])
            else:
                S.add("dve", MSET(zeros[:], 0.0), writes=["zeros"])
            S.add("dve", MSET(hcar[:], 0.0), writes=["hcar"])
            S.add("dve", MSET(acar[:], 1.0), writes=["acar"])
            S.dma("pool", DMA(gate_b[:], gatebd_d.rearrange("p (a b) -> p a b", a=8)), writes=["gate_b"], rr=("lq", 4))
            S.dma("pool", DMA(bd64_b[:], consts_d[:, 256:384]), writes=["bd64_b"], rr=("lq", 4))
            ev, tv = lt
            S.add("act", ACT(ev[:], V("lam", 4), AF.Exp, scale=-1.0), reads=["vecs"], writes=["lt0"])
            S.add("dve", TS(tv[:], ev[:], 0.2, -0.25, ALU.mult, ALU.add), reads=["lt0"], writes=["lt1"])
            for cc in (1.0 / 3.0, -0.5, 1.0):
                S.add("dve", TT(tv[:], tv[:], ev[:], ALU.mult), reads=["lt0", "lt1"], writes=["lt1"])
                S.add("dve", TS1(tv[:], tv[:], cc, ALU.add), reads=["lt1"], writes=["lt1"])
            S.add("dve", TT(tv[:], tv[:], ev[:], ALU.mult), reads=["lt0", "lt1"], writes=["lt1"])
            S.add("dve", TS1(c8[:], tv[:], -8.0, ALU.mult), reads=["lt1"], writes=["c8"])
            S.add("dve", TS1(c16[:], tv[:], -16.0, ALU.mult), reads=["lt1"], writes=["c16"])
        if doB:
            S.dma("pool", DMA(ident_b[:], consts_d[:, 0:128]), writes=["ident_b"], rr=("lq", 4))
            S.dma("sp", DMA(tril[:], consts_d[:, 128:256]), writes=["tril"], rr=("ld", 8))
            S.dma("sp", DMA(ybuf[:, 0:2, :], wsT_d.rearrange("p (a b) -> p a b", a=2)), writes=[("yb", 0), ("yb", 1)], rr=("ld", 8))
            for h in range(8):
                S.add("dve", TT(wsT_b[:, h, :], ybuf[:, h // 4, (h % 4) * 128:(h % 4 + 1) * 128], tril[:], ALU.mult),
                      reads=[("yb", h // 4), "tril"], writes=["wsT_b"])
            S.dma("sp", DMA(bsb[:], bs_d.partition_broadcast(128).rearrange("p o (a b) -> p (o a) b", a=8)), writes=["bsb"], rr=("ld", 8))

        S.add("act", ACT(cact[:], V("c", 8), AF.Silu), reads=["vecs"], writes=["cact"])
        S.add("dve", CP(cact_b[:], cact[:]), reads=["cact"], writes=["cact_b"])
        ada_st = [act[:, 0:4, :].rearrange("p a b -> p (a b)")[:, 0:4096].rearrange("p (k n) -> p k n", k=8),
                  act[:, 4:8, :].rearrange("p a b -> p (a b)")[:, 0:4096].rearrange("p (k n) -> p k n", k=8)]
        layers = [0, 1] if doB else [0]
        for l in layers:
            for ct in range(18):
                if (not doB) and ct >= 12:
                    continue
                si = rotate("ada", ada_st)
                slot = ada_st[si]
                skey = ("adast", si)
                for hf in range(2):
                    S.dma("pool", DMA(slot[:, hf * 4:(hf + 1) * 4, :],
                                      adaw_d[l * 18 + ct, :, hf * 2048:(hf + 1) * 2048].rearrange("p (k n) -> p k n", k=4)),
                          writes=[skey], semkey="ada%d" % si)
                bi = rotate("brow", [0, 1])
                browt, kbrow = tmpf[bi][0:1, :], ("tmpf", bi)
                S.dma("sp", DMA(browt, adab_d[l * 18 + ct:l * 18 + ct + 1, :]), writes=[kbrow], rr=("ld", 8))
                pi = nextps()
                S.add("pe", [MM(ps[pi][0:1, :], cact_b[:, k:k + 1], slot[:, k, :], start=(k == 0), stop=(k == 7)) for k in range(8)],
                      reads=[skey, "cact_b"], writes=[("ps", pi)])
                mi = 2
                mrowt, kmrow = tmpf[mi][0:1, :], ("tmpf", mi)
                S.add("dve", TT(mrowt, ps[pi][0:1, :], browt, ALU.add), reads=[("ps", pi), kbrow], writes=[kmrow])
                pj = nextps()
                S.add("pe", [MM(ps[pj][:, q:q + 1], mrowt[:, q * 128:(q + 1) * 128], one_f[0:1, 0:1]) for q in range(4)],
                      reads=[kmrow, "one_f"], writes=[("ps", pj)])
                S.add("dve", CP(modT[:, l * 72 + ct * 4:l * 72 + ct * 4 + 4], ps[pj][:, 0:4]), reads=[("ps", pj)], writes=["modT"])
        S.alias([("adast", 0), ("adast", 1)], [("act", fc, j) for fc in range(FC) for j in range(3)])

        def modcol(l, sub, kind):
            return l * 72 + (sub * 3 + kind) * 8

        for l in layers:
            for sub in range(3):
                o = (l * 3 + sub) * 8
                m1 = modcol(l, sub, 1)
                m2 = modcol(l, sub, 2)
                S.add("dve", STT(gsT[:, o:o + 8], modT[:, m1:m1 + 8], 1.0, V(("npre", l, sub), 8), ALU.add, ALU.mult),
                      reads=["modT", "vecs"], writes=["gsT"])
                S.add("dve", STT(coefT[:, o:o + 8], modT[:, m2:m2 + 8], 1.0, V(("npost", l, sub), 8), ALU.add, ALU.mult),
                      reads=["modT", "vecs"], writes=["coefT"])
                if sub != 1:
                    S.add("dve", TS1(coefT[:, o:o + 8], coefT[:, o:o + 8], 0.5, ALU.mult), reads=["coefT"], writes=["coefT"])

        def jkey(c0):
            return 0 if c0 == 0 else (1 if c0 == HALO else 2)

        def prenorm(l, sub, subtiles):
            o = (l * 3 + sub) * 8
            sh = modcol(l, sub, 0)
            for (c0, n) in subtiles:
                j = jkey(c0)
                sp_i = 5 + j
                for c in range(8):
                    qi = rotate("sqb", sqb)
                    S.add("act", ACT(sqb[qi][:, :n], xres[:, c, c0:c0 + n], AF.Square), reads=[("xres", c, j)], writes=[("sqb", qi)])
                    S.add("pe", MM(ps[sp_i][:, :n], ones_b[:], sqb[qi][:, :n], start=(c == 0), stop=(c == 7)),
                          reads=[("sqb", qi), "ones_b"], writes=[("ps", sp_i)])
                S.add("act", ACT(rstd[:, c0:c0 + n], ps[sp_i][:, :n], AF.Sqrt, bias=EPSAP, scale=1.0 / D),
                      reads=[("ps", sp_i), "cst"], writes=[("rstd", j)])
                S.add("dve", RCP(rstd[:, c0:c0 + n], rstd[:, c0:c0 + n]), reads=[("rstd", j)], writes=[("rstd", j)])
                for c in range(8):
                    ti = rotate("tmpf", tmpf)
                    S.add("dve", TT(tmpf[ti][:, :n], xres[:, c, c0:c0 + n], rstd[:, c0:c0 + n], ALU.mult),
                          reads=[("xres", c, j), ("rstd", j)], writes=[("tmpf", ti)])
                    S.add("act", ACT(hb[:, c, c0:c0 + n], tmpf[ti][:, :n], AF.Identity, bias=modT[:, sh + c:sh + c + 1],
                                     scale=gsT[:, o + c:o + c + 1]),
                          reads=[("tmpf", ti), "modT", "gsT"], writes=[("hb", c, j)])

        def ydst(dc, c0, n):
            if c0 == 0:
                return yhalo[:, dc, 0:n], ("yh", dc)
            jj = 0 if c0 == HALO else 1
            return ybuf[:, dc * 2 + jj, 0:n], ("yb", dc * 2 + jj)

        def post_evac(pi, dc, c0, n):
            j = jkey(c0)
            sp_i = 5 + j
            dst, dkey = ydst(dc, c0, n)
            S.add("act", ACT(dst, ps[pi][:, :n], AF.Copy), reads=[("ps", pi)], writes=[dkey])
            qi = rotate("sqb", sqb)
            S.add("act", ACT(sqb[qi][:, :n], ps[pi][:, :n], AF.Square), reads=[("ps", pi)], writes=[("sqb", qi)])
            flush_stats()
            pend_stats.append((sp_i, qi, n, dc))

        pend_stats = []

        def flush_stats():
            while pend_stats:
                sp_i, qi, n, dc = pend_stats.pop(0)
                S.add("pe", MM(ps[sp_i][:, :n], ones_b[:], sqb[qi][:, :n], start=(dc == 0), stop=(dc == 7)),
                      reads=[("sqb", qi), "ones_b"], writes=[("ps", sp_i)])

        def post_apply(l, sub, subtiles):
            flush_stats()
            o = (l * 3 + sub) * 8
            for (c0, n) in subtiles:
                j = jkey(c0)
                sp_i = 5 + j
                S.add("act", ACT(rstd2[:, c0:c0 + n], ps[sp_i][:, :n], AF.Sqrt, bias=EPSAP, scale=1.0 / D),
                      reads=[("ps", sp_i), "cst"], writes=[("rstd2", j)])
                S.add("dve", RCP(rstd2[:, c0:c0 + n], rstd2[:, c0:c0 + n]), reads=[("rstd2", j)], writes=[("rstd2", j)])
                for dc in range(8):
                    src, skey = ydst(dc, c0, n)
                    S.add("dve", TT(src, src, rstd2[:, c0:c0 + n], ALU.mult), reads=[skey, ("rstd2", j)], writes=[skey])
                    S.add("dve", STT(xres[:, dc, c0:c0 + n], src, coefT[:, o + dc:o + dc + 1], xres[:, dc, c0:c0 + n], ALU.mult, ALU.add),
                          reads=[skey, "coefT", ("xres", dc, j)], writes=[("xres", dc, j)])

        def load_w13(src_ap):
            si = rotate("w13s", w13s)
            S.dma("pool", DMA(w13s[si][:], src_ap.rearrange("p (k n) -> p k n", k=8)), writes=[("w13s", si)], semkey="w13s%d" % si)
            return si

        def load_w2(src_ap, nk):
            si = rotate("w2s", w2s)
            h = nk // 2
            for a, b in ((0, h), (h, nk)):
                S.dma("pool", DMA(w2s[si][:, a:b, :], src_ap[:, a * 128:b * 128].rearrange("p (k n) -> p k n", n=128)),
                      writes=[("w2s", si)], semkey="w2s%d" % si)
            return si

        def proj8(si, half, c0, n, pi=None):
            j = jkey(c0)
            pi = nextps() if pi is None else pi
            S.add("pe", [MM(ps[pi][:, :n], w13s[si][:, k, half * 128:(half + 1) * 128], hb[:, k, c0:c0 + n],
                            start=(k == 0), stop=(k == 7)) for k in range(8)],
                  reads=[("w13s", si)] + [("hb", k, j) for k in range(8)], writes=[("ps", pi)])
            return pi

        def ffn(l, f, subtiles, pre=True, post=True):
            sub = 0 if f == 0 else 2
            if pre:
                prenorm(l, sub, subtiles)
            for fc in range(FC):
                si = load_w13(w13_d[(l * 2 + f) * 22 + fc])
                for (c0, n) in subtiles:
                    j = jkey(c0)
                    gi = proj8(si, 0, c0, n)
                    ui = proj8(si, 1, c0, n)
                    ti = rotate("tmpf", tmpf)
                    S.add("act", ACT(tmpf[ti][:, :n], ps[gi][:, :n], AF.Silu), reads=[("ps", gi)], writes=[("tmpf", ti)])
                    S.add("dve", TT(act[:, fc, c0:c0 + n], tmpf[ti][:, :n], ps[ui][:, :n], ALU.mult),
                          reads=[("tmpf", ti), ("ps", ui)], writes=[("act", fc, j)])
                    pump(PUMP_N)
            drain()
            for dc in range(8):
                si = load_w2(w2_d[(l * 2 + f) * 8 + dc], FC)
                for (c0, n) in subtiles:
                    j = jkey(c0)
                    pi = nextps()
                    S.add("pe", [MM(ps[pi][:, :n], w2s[si][:, k, :], act[:, k, c0:c0 + n], start=(k == 0), stop=(k == FC - 1))
                                 for k in range(FC)],
                          reads=[("w2s", si)] + [("act", k, j) for k in range(FC)], writes=[("ps", pi)])
                    post_evac(pi, dc, c0, n)
            if post:
                post_apply(l, sub, subtiles)

        def trans(lp, subp, ln, subn, subtiles):
            for st in subtiles:
                post_apply(lp, subp, [st])
                if ln is not None:
                    prenorm(ln, subn, [st])

        def out_proj(src_d, l, subtiles, src=None, srckey="hb", post=True):
            src = hb if src is None else src
            for dc in range(8):
                si = load_w2(src_d[dc], 8)
                for (c0, n) in subtiles:
                    j = jkey(c0)
                    pi = nextps()
                    S.add("pe", [MM(ps[pi][:, :n], w2s[si][:, k, :], src[:, k, c0:c0 + n], start=(k == 0), stop=(k == 7))
                                 for k in range(8)],
                          reads=[("w2s", si)] + [(srckey, k, j) for k in range(8)], writes=[("ps", pi)])
                    post_evac(pi, dc, c0, n)
            if post:
                post_apply(l, 1, subtiles)

        def blk(i):
            return ybuf[:, i, :], ("yb", i)

        import collections
        bgq = collections.deque()

        def pump(k):
            while k > 0 and bgq:
                try:
                    next(bgq[0])
                    k -= 1
                except StopIteration:
                    bgq.popleft()

        def drain():
            pump(1 << 30)

        def mixer_a_fast(t, subtiles, mode):
            drain()
            mcol = Vc("mpre", t) if mode == "prefix" else None
            psrot["pool"] = [0, 1, 2, 3] if mode == "direct" else [0, 1, 2, 3, 4, 5, 6, 7]

            def axr(c, jj):
                return AX[c][:, AXH + jj * 512:AXH + (jj + 1) * 512], ("AX", c, jj)

            def scr(c, jj):
                if mode == "prefix":
                    return axr(c, jj)
                return blk(c * 4 + (1 - jj) * 2)

            def project(c0, n, cs=range(4), want_gate=False):
                agi = {}
                j = jkey(c0)
                for c in cs:
                    si = load_w13(abin_d[c])
                    axi = proj8(si, 0, c0, n)
                    if c0 == 0:
                        S.add("dve", TS1(AX[c][:, 0:AXH], ps[axi][:, HALO - AXH:HALO], V("mhalo"), ALU.mult),
                              reads=[("ps", axi), "vecs"], writes=[("AX", c, "h")])
                        continue
                    if want_gate:
                        agi[c] = proj8(si, 1, c0, n, pi=4 + c)
                    reg, kreg = axr(c, j - 1)
                    S.add("act", ACT(reg, ps[axi][:, :], AF.Copy), reads=[("ps", axi)], writes=[kreg])
                return agi

            def chains(lanes, agi=None, c0n=None):
                BA = {ln: blk(ln[0] * 4 + ln[1] * 2) for ln in lanes}
                BB = {ln: blk(ln[0] * 4 + ln[1] * 2 + 1) for ln in lanes}
                def convkeys(c, jj):
                    return [("AX", c, jj), ("AX", c, jj - 1) if jj > 0 else ("AX", c, "h")]
                for (c, jj) in lanes:
                    a0 = AXH + jj * 512
                    XR, kXR = BA[(c, jj)]
                    S.add("dve", TS(XR, AX[c][:, a0:a0 + 512], Vc(("acw", 3), c), Vc("acb", c), ALU.mult, ALU.add),
                          reads=convkeys(c, jj) + ["vecs"], writes=[kXR])
                for k in range(3):
                    for (c, jj) in lanes:
                        a0 = AXH + jj * 512
                        XR, kXR = BA[(c, jj)]
                        S.add("dve", STT(XR, AX[c][:, a0 - 3 + k:a0 - 3 + k + 512], Vc(("acw", k), c), XR, ALU.mult, ALU.add),
                              reads=convkeys(c, jj) + ["vecs", kXR], writes=[kXR])
                for (c, jj) in lanes:
                    if jj == 1:
                        if mode == "prefix":
                            S.add("dve", TS1(AX[c][:, 0:AXH], AX[c][:, TILE:TILE + AXH], mcol, ALU.mult),
                                  reads=[("AX", c, 1), "vecs"], writes=[("AX", c, "h")])
                        else:
                            S.add("dve", CP(AX[c][:, 0:AXH], AX[c][:, TILE:TILE + AXH]), reads=[("AX", c, 1)], writes=[("AX", c, "h")])
                for h0 in range(0, len(lanes), 4):
                    grp = lanes[h0:h0 + 4]
                    xis = {}
                    for ln in grp:
                        xi = rotate("xrb", xrb)
                        xis[ln] = xi
                        S.add("act", ACT(xrb[xi][:], BA[ln][0], AF.Copy), reads=[BA[ln][1]], writes=[("xrb", xi)])
                    for ln in grp:
                        c, jj = ln
                        reg, kreg = scr(c, jj)
                        ri = nextps()
                        S.add("pe", MM(ps[ri][:, :], gate_b[:, c * 2, :], xrb[xis[ln]][:]), reads=[("xrb", xis[ln]), "gate_b"], writes=[("ps", ri)])
                        S.add("act", ACT(BB[ln][0], ps[ri][:, :], AF.Sigmoid, bias=Vc("gbr", c)), reads=[("ps", ri), "vecs"], writes=[BB[ln][1]])
                        ii = nextps()
                        S.add("pe", MM(ps[ii][:, :], gate_b[:, c * 2 + 1, :], xrb[xis[ln]][:]), reads=[("xrb", xis[ln]), "gate_b"], writes=[("ps", ii)])
                        S.add("act", ACT(reg, ps[ii][:, :], AF.Sigmoid, bias=Vc("gbi", c)), reads=[("ps", ii), "vecs"], writes=[kreg])
                for ln in lanes:
                    c, jj = ln
                    reg, kreg = scr(c, jj)
                    if mode == "prefix":
                        S.add("dve", STT(BA[ln][0], reg, mcol, BA[ln][0], ALU.mult, ALU.mult), reads=[kreg, BA[ln][1], "vecs"], writes=[BA[ln][1]])
                    else:
                        S.add("dve", TT(BA[ln][0], reg, BA[ln][0], ALU.mult), reads=[kreg, BA[ln][1]], writes=[BA[ln][1]])
                for ln in lanes:
                    c, jj = ln
                    reg, kreg = scr(c, jj)
                    S.add("act", ACT(reg, BB[ln][0], AF.Exp, scale=c8[:, c:c + 1]), reads=[BB[ln][1], "c8"], writes=[kreg])
                    S.add("act", ACT(BB[ln][0], BB[ln][0], AF.Exp, scale=c16[:, c:c + 1]), reads=[BB[ln][1], "c16"], writes=[BB[ln][1]])
                for ln in lanes:
                    S.add("act", ACT(BB[ln][0], BB[ln][0], AF.Sqrt, bias=ONEAP, scale=-1.0), reads=[BB[ln][1], "cst"], writes=[BB[ln][1]])
                for ln in lanes:
                    S.add("dve", TT(BB[ln][0], BB[ln][0], BA[ln][0], ALU.mult), reads=[BB[ln][1], BA[ln][1]], writes=[BB[ln][1]])
                for ln in sorted(lanes):
                    c, jj = ln
                    reg, kreg = scr(c, jj)
                    Hh, kH = BA[ln]
                    S.add("dve", SCAN(Hh, reg, BB[ln][0], hcar[:, c:c + 1]), reads=[kreg, BB[ln][1], "hcar"], writes=[kH])
                    S.add("dve", CP(hcar[:, c:c + 1], Hh[:, 511:512]), reads=[kH], writes=["hcar"])
                if mode == "direct":
                    c0, n = c0n
                    for ln in lanes:
                        c, jj = ln
                        reg, kreg = scr(c, jj)
                        S.add("act", ACT(reg, ps[agi[c]][:, :], AF.Gelu_apprx_tanh), reads=[("ps", agi[c])], writes=[kreg])
                        S.add("dve", TT(act[:, c, c0:c0 + n], BA[ln][0], reg, ALU.mult), reads=[BA[ln][1], kreg], writes=[("act", c, jj + 1)])

            if mode == "prefix":
                for (c0, n) in subtiles:
                    project(c0, n)
                chains([(c, jj) for jj in range(2) for c in range(4)])
            else:
                for (c0, n) in subtiles:
                    agi = project(c0, n, want_gate=True)
                    if c0 == 0:
                        continue
                    chains([(c, jkey(c0) - 1) for c in range(4)], agi=agi, c0n=(c0, n))
            psrot["pool"] = [0, 1, 2, 3, 4]

        def mixer_b_fast(t, subtiles):
            free_keys = [("act", fc, j) for fc in range(8, FC) for j in range(3)]
            new_keys = [("dg", 0), ("dg", 1)] + [("VBb", c) for c in range(4)]
            S.alias(free_keys, new_keys)
            dgv = act[:, 8:16, :].rearrange("p a b -> p (a b)")[:, 0:7936].rearrange("p (s k n) -> p s k n", s=2, k=31)
            vbv = act[:, 16:22, :].rearrange("p a b -> p (a b)")[:, 0:4 * WCOL].rearrange("p (c n) -> p c n", c=4)
            for c in range(4):
                si = load_w13(abin_d[4 + c])
                kVB = ("VBb", c)
                if t > 0:
                    S.add("dve", CP(vbv[:, c, 0:HALO], vhalo[:, c, :]), reads=["vhalo"], writes=[kVB])
                for (c0, n) in subtiles:
                    bvi = proj8(si, 0, c0, n)
                    bgi = proj8(si, 1, c0, n)
                    ti = rotate("tmpf", tmpf)
                    SG, kSG = tmpf[ti], ("tmpf", ti)
                    S.add("act", ACT(SG[:, :n], ps[bgi][:, :n], AF.Sigmoid), reads=[("ps", bgi)], writes=[kSG])
                    if c0 == 0:
                        S.add("dve", TT(SG[:, :HALO], ps[bvi][:, :HALO], SG[:, :HALO], ALU.mult), reads=[("ps", bvi), kSG], writes=[kSG])
                        S.add("dve", TS1(vbv[:, c, 0:HALO], SG[:, :HALO], V("mhalo"), ALU.mult), reads=[kSG, "vecs"], writes=[kVB])
                    else:
                        S.add("dve", TT(vbv[:, c, c0:c0 + n], ps[bvi][:, :n], SG[:, :n], ALU.mult), reads=[("ps", bvi), kSG], writes=[kVB])
                ds = c % 2
                for k in range(31):
                    if k % 2 == 0:
                        S.add("dve", TS1(dgv[:, ds, k, :], ident_b[:], Vc(("bcw", k), c), ALU.mult), reads=["ident_b", "vecs"], writes=[("dg", ds)])
                    else:
                        S.add("act", ACT(dgv[:, ds, k, :], ident_b[:], AF.Identity, scale=Vc(("bcw", k), c)), reads=["ident_b", "vecs"], writes=[("dg", ds)])
                for (c0, n) in subtiles:
                    if c0 == 0:
                        continue
                    j = jkey(c0)
                    jj = j - 1
                    cvi = nextps()
                    S.add("pe", [MM(ps[cvi][:, :n], dgv[:, ds, k, :], vbv[:, c, c0 - 30 + k:c0 - 30 + k + n], start=(k == 0), stop=(k == 30))
                                 for k in range(31)], reads=[("dg", ds), kVB], writes=[("ps", cvi)])
                    CV, kCV = blk(1 + jj * 8)
                    MN, kMN = blk(3 + jj * 8)
                    VR, kVR = blk(4 + jj * 8)
                    S.add("act", ACT(CV, ps[cvi][:, :n], AF.Identity, bias=Vc("bcb", c)), reads=[("ps", cvi), "vecs"], writes=[kCV])
                    q1 = rotate("sqb", sqb)
                    S.add("act", ACT(sqb[q1][:], CV, AF.Copy), reads=[kCV], writes=[("sqb", q1)])
                    mi_ = nextps()
                    S.add("pe", MM(ps[mi_][:, :], bd64_b[:], sqb[q1][:]), reads=[("sqb", q1), "bd64_b"], writes=[("ps", mi_)])
                    q2 = rotate("sqb", sqb)
                    S.add("act", ACT(sqb[q2][:], CV, AF.Square), reads=[kCV], writes=[("sqb", q2)])
                    vi_ = nextps()
                    S.add("pe", MM(ps[vi_][:, :], bd64_b[:], sqb[q2][:]), reads=[("sqb", q2), "bd64_b"], writes=[("ps", vi_)])
                    S.add("act", ACT(MN, ps[mi_][:, :], AF.Copy, scale=1.0 / 64), reads=[("ps", mi_)], writes=[kMN])
                    S.add("dve", TT(VR, MN, MN, ALU.mult), reads=[kMN], writes=[kVR])
                    S.add("dve", STT(VR, ps[vi_][:, :], 1.0 / 64, VR, ALU.mult, ALU.subtract), reads=[("ps", vi_), kVR], writes=[kVR])
                    S.add("act", ACT(VR, VR, AF.Sqrt, bias=EPSAP, scale=1.0), reads=[kVR, "cst"], writes=[kVR])
                    S.add("dve", RCP(VR, VR), reads=[kVR], writes=[kVR])
                    S.add("dve", TT(CV, CV, MN, ALU.subtract), reads=[kCV, kMN], writes=[kCV])
                    S.add("dve", TT(CV, CV, VR, ALU.mult), reads=[kCV, kVR], writes=[kCV])
                    S.add("act", ACT(act[:, 4 + c, c0:c0 + n], CV, AF.Silu, bias=Vc("bnb", c), scale=Vc("bng", c)),
                          reads=[kCV, "vecs"], writes=[("act", 4 + c, j)])
                S.add("dve", CP(vhalo[:, c, :], vbv[:, c, TILE:TILE + HALO]), reads=[kVB], writes=["vhalo"])
            S.alias(new_keys, free_keys)

        def mixer_ab_in(t, subtiles, mode="spill", pre=True):
            tok0 = t * TILE
            mcol = Vc("mpre", t) if mode == "prefix" else None
            if pre:
                prenorm(0, 1, subtiles)
            if mode != "spill":
                mixer_a_fast(t, subtiles, mode)
            for c in (range(4) if mode == "spill" else ()):
                si = load_w13(abin_d[c])
                for (c0, n) in subtiles:
                    axi = proj8(si, 0, c0, n)
                    if c0 == 0:
                        S.add("dve", TS1(AX[c][:, 0:HALO], ps[axi][:, 0:HALO], V("mhalo"), ALU.mult),
                              reads=[("ps", axi), "vecs"], writes=[("AX", c)])
                        continue
                    agi = proj8(si, 1, c0, n) if mode != "prefix" else None
                    jj = jkey(c0) - 1
                    XR, kXR = blk(0 + jj * 8)
                    Rb, kR = blk(1 + jj * 8)
                    Ib, kI = blk(2 + jj * 8)
                    Aa, kA = blk(3 + jj * 8)
                    Sq, kS = blk(4 + jj * 8)
                    Hh, kH = blk(5 + jj * 8)
                    At, kAt = blk(6 + jj * 8)
                    GL, kG = blk(7 + jj * 8)
                    AXc = AX[c]
                    kAX = ("AX", c)
                    S.add("act", ACT(AXc[:, HALO:HALO + 512], ps[axi][:, :], AF.Copy), reads=[("ps", axi)], writes=[kAX])
                    S.add("dve", TS(XR, AXc[:, HALO:HALO + 512], Vc(("acw", 3), c), Vc("acb", c), ALU.mult, ALU.add),
                          reads=[kAX, "vecs"], writes=[kXR])
                    for k in range(3):
                        S.add("dve", STT(XR, AXc[:, HALO - 3 + k:HALO - 3 + k + 512], Vc(("acw", k), c), XR, ALU.mult, ALU.add),
                              reads=[kAX, "vecs", kXR], writes=[kXR])
                    if mode == "prefix":
                        S.add("pool", TS1(AXc[:, 0:HALO], AXc[:, 512:512 + HALO], mcol, ALU.mult), reads=[kAX, "vecs"], writes=[kAX])
                    else:
                        S.add("pool", CP(AXc[:, 0:HALO], AXc[:, 512:512 + HALO]), reads=[kAX], writes=[kAX])
                    xi = rotate("xrb", xrb)
                    S.add("act", ACT(xrb[xi][:], XR, AF.Copy), reads=[kXR], writes=[("xrb", xi)])
                    ri = nextps()
                    S.add("pe", MM(ps[ri][:, :], gate_b[:, c * 2, :], xrb[xi][:]), reads=[("xrb", xi), "gate_b"], writes=[("ps", ri)])
                    ii = nextps()
                    S.add("pe", MM(ps[ii][:, :], gate_b[:, c * 2 + 1, :], xrb[xi][:]), reads=[("xrb", xi), "gate_b"], writes=[("ps", ii)])
                    S.add("act", ACT(Rb, ps[ri][:, :], AF.Sigmoid, bias=Vc("gbr", c)), reads=[("ps", ri), "vecs"], writes=[kR])
                    S.add("act", ACT(Ib, ps[ii][:, :], AF.Sigmoid, bias=Vc("gbi", c)), reads=[("ps", ii), "vecs"], writes=[kI])
                    S.add("act", ACT(Aa, Rb, AF.Exp, scale=c8[:, c:c + 1]), reads=[kR, "c8"], writes=[kA])
                    S.add("act", ACT(Sq, Rb, AF.Exp, scale=c16[:, c:c + 1]), reads=[kR, "c16"], writes=[kS])
                    S.add("act", ACT(Sq, Sq, AF.Sqrt, bias=ONEAP, scale=-1.0), reads=[kS, "cst"], writes=[kS])
                    S.add("pool", TT(Sq, Sq, Ib, ALU.mult), reads=[kS, kI], writes=[kS])
                    if mode == "prefix":
                        S.add("dve", STT(Sq, Sq, mcol, XR, ALU.mult, ALU.mult), reads=[kS, kXR, "vecs"], writes=[kS])
                    else:
                        S.add("pool", TT(Sq, Sq, XR, ALU.mult), reads=[kS, kXR], writes=[kS])
                    S.add("dve", SCAN(Hh, Aa, Sq, hcar[:, c:c + 1]), reads=[kA, kS, "hcar"], writes=[kH])
                    S.add("dve", CP(hcar[:, c:c + 1], Hh[:, 511:512]), reads=[kH], writes=["hcar"])
                    if mode == "prefix":
                        continue
                    S.add("act", ACT(GL, ps[agi][:, :], AF.Gelu_apprx_tanh), reads=[("ps", agi)], writes=[kG])
                    if mode == "direct":
                        S.add("pool", TT(act[:, c, c0:c0 + n], Hh, GL, ALU.mult), reads=[kH, kG], writes=[("act", c, jkey(c0))])
                        continue
                    S.add("dve", SCAN(At, Aa, zeros[:], acar[:, c:c + 1]), reads=[kA, "zeros", "acar"], writes=[kAt])
                    S.add("dve", CP(acar[:, c:c + 1], At[:, 511:512]), reads=[kAt], writes=["acar"])
                    S.add("pool", TT(Hh, Hh, GL, ALU.mult), reads=[kH, kG], writes=[kH])
                    S.add("pool", TT(At, At, GL, ALU.mult), reads=[kAt, kG], writes=[kAt])
                    d0 = tok0 + c0 - HALO
                    S.dma("sp", DMA(P_d[:, c, d0:d0 + 512], Hh), reads=[kH], writes=[("dP", c, t)], rr=("st", 8))
                    S.dma("sp", DMA(Q_d[:, c, d0:d0 + 512], At), reads=[kAt], writes=[("dQ", c, t)], rr=("st", 8))
            if mode == "prefix":
                return
            if mode == "direct":
                mixer_b_fast(t, subtiles)
                return
            for c in range(4):
                si = load_w13(abin_d[4 + c])
                for (c0, n) in subtiles:
                    bvi = proj8(si, 0, c0, n)
                    bgi = proj8(si, 1, c0, n)
                    jj = max(jkey(c0) - 1, 0)
                    SG, kSG = blk(0 + jj * 8)
                    CV, kCV = blk(1 + jj * 8)
                    CV2, kCV2 = blk(2 + jj * 8)
                    MN, kMN = blk(3 + jj * 8)
                    VR, kVR = blk(4 + jj * 8)
                    VBc = VB[c]
                    kVB = ("VB", c)
                    S.add("act", ACT(SG[:, :n], ps[bgi][:, :n], AF.Sigmoid), reads=[("ps", bgi)], writes=[kSG])
                    if c0 == 0:
                        S.add("dve", TT(SG[:, :HALO], ps[bvi][:, :HALO], SG[:, :HALO], ALU.mult), reads=[("ps", bvi), kSG], writes=[kSG])
                        S.add("dve", TS1(VBc[:, 0:HALO], SG[:, :HALO], V("mhalo"), ALU.mult), reads=[kSG, "vecs"], writes=[kVB])
                        continue
                    S.add("dve", TT(VBc[:, HALO:HALO + 512], ps[bvi][:, :], SG, ALU.mult), reads=[("ps", bvi), kSG], writes=[kVB])
                    S.add("dve", TS(CV, VBc[:, HALO:HALO + 512], Vc(("bcw", 30), c), Vc("bcb", c), ALU.mult, ALU.add),
                          reads=[kVB, "vecs"], writes=[kCV])
                    for k in range(0, 18):
                        S.add("dve", STT(CV, VBc[:, HALO - 30 + k:HALO - 30 + k + 512], Vc(("bcw", k), c), CV, ALU.mult, ALU.add),
                              reads=[kVB, "vecs", kCV], writes=[kCV])
                    S.add("act", ACT(CV2, VBc[:, HALO - 30 + 18:HALO - 30 + 18 + 512], AF.Identity, scale=Vc(("bcw", 18), c)),
                          reads=[kVB, "vecs"], writes=[kCV2])
                    for k in range(19, 30):
                        PB, kPB = blk(5 + (k % 2) + jj * 8)
                        S.add("act", ACT(PB, VBc[:, HALO - 30 + k:HALO - 30 + k + 512], AF.Identity, scale=Vc(("bcw", k), c)),
                              reads=[kVB, "vecs"], writes=[kPB])
                        S.add("pool", TT(CV2, CV2, PB, ALU.add), reads=[kCV2, kPB], writes=[kCV2])
                    S.add("pool", TT(CV, CV, CV2, ALU.add), reads=[kCV, kCV2], writes=[kCV])
                    S.add("pool", CP(VBc[:, 0:HALO], VBc[:, 512:512 + HALO]), reads=[kVB], writes=[kVB])
                    q1 = rotate("sqb", sqb)
                    S.add("act", ACT(sqb[q1][:], CV, AF.Copy), reads=[kCV], writes=[("sqb", q1)])
                    mi_ = nextps()
                    S.add("pe", MM(ps[mi_][:, :], bd64_b[:], sqb[q1][:]), reads=[("sqb", q1), "bd64_b"], writes=[("ps", mi_)])
                    q2 = rotate("sqb", sqb)
                    S.add("act", ACT(sqb[q2][:], CV, AF.Square), reads=[kCV], writes=[("sqb", q2)])
                    vi_ = nextps()
                    S.add("pe", MM(ps[vi_][:, :], bd64_b[:], sqb[q2][:]), reads=[("sqb", q2), "bd64_b"], writes=[("ps", vi_)])
                    S.add("act", ACT(MN, ps[mi_][:, :], AF.Copy, scale=1.0 / 64), reads=[("ps", mi_)], writes=[kMN])
                    S.add("dve", TT(VR, MN, MN, ALU.mult), reads=[kMN], writes=[kVR])
                    S.add("dve", STT(VR, ps[vi_][:, :], 1.0 / 64, VR, ALU.mult, ALU.subtract), reads=[("ps", vi_), kVR], writes=[kVR])
                    S.add("act", ACT(VR, VR, AF.Sqrt, bias=EPSAP, scale=1.0), reads=[kVR, "cst"], writes=[kVR])
                    S.add("dve", RCP(VR, VR), reads=[kVR], writes=[kVR])
                    S.add("dve", TT(CV, CV, MN, ALU.subtract), reads=[kCV, kMN], writes=[kCV])
                    S.add("dve", TT(CV, CV, VR, ALU.mult), reads=[kCV, kVR], writes=[kCV])
                    if mode == "direct":
                        S.add("act", ACT(act[:, 4 + c, c0:c0 + n], CV, AF.Silu, bias=Vc("bnb", c), scale=Vc("bng", c)),
                              reads=[kCV, "vecs"], writes=[("act", 4 + c, jkey(c0))])
                        continue
                    S.add("act", ACT(CV, CV, AF.Silu, bias=Vc("bnb", c), scale=Vc("bng", c)), reads=[kCV, "vecs"], writes=[kCV])
                    d0 = tok0 + c0 - HALO
                    S.dma("sp", DMA(yb_d[:, c, d0:d0 + 512], CV), reads=[kCV], writes=[("dY", c, t)], rr=("st", 8))

        def mixer_c(subtiles, pre=True, post=True):
            if pre:
                prenorm(1, 1, subtiles)
            psrot["pool"] = [0, 1, 2, 3]
            vf = act[:, 0:16, :].rearrange("p a b -> p (a b)")[:, 0:16384].bitcast(F32).rearrange("p (a b) -> p a b", a=16)
            actkeys = [("act", fc, j) for fc in range(FC) for j in range(3)]
            vkeys = [("vf", i) for i in range(16)]
            S.alias(actkeys, vkeys)
            for c in range(8):
                si = load_w13(cin_d[c])
                for (c0, n) in subtiles:
                    jj = jkey(c0) - 1
                    ui = proj8(si, 0, c0, n)
                    vi = proj8(si, 1, c0, n)
                    U, kU = blk(c * 2 + jj)
                    Vv, kV = vf[:, c * 2 + jj, :], ("vf", c * 2 + jj)
                    S.add("act", ACT(U, ps[ui][:, :], AF.Gelu_apprx_tanh, bias=Vc("cbu", c)), reads=[("ps", ui), "vecs"], writes=[kU])
                    S.add("act", ACT(Vv, ps[vi][:, :], AF.Gelu_apprx_tanh, bias=Vc("cbv", c)), reads=[("ps", vi), "vecs"], writes=[kV])
                    q1 = rotate("sqb", sqb)
                    S.add("act", ACT(sqb[q1][:], Vv, AF.Copy), reads=[kV], writes=[("sqb", q1)])
                    S.add("pe", MM(ps[4 + jj][:, :], ones_b[:], sqb[q1][:], start=(c == 0), stop=(c == 7)),
                          reads=[("sqb", q1), "ones_b"], writes=[("ps", 4 + jj)])
                    q2 = rotate("sqb", sqb)
                    S.add("dve", TT(sqb[q2][:], Vv, Vv, ALU.mult), reads=[kV], writes=[("sqb", q2)])
                    S.add("pe", MM(ps[6 + jj][:, :], ones_b[:], sqb[q2][:], start=(c == 0), stop=(c == 7)),
                          reads=[("sqb", q2), "ones_b"], writes=[("ps", 6 + jj)])
            for (c0, n) in subtiles:
                j = jkey(c0)
                jj = j - 1
                MN = rstd[:, c0:c0 + n]
                RS = rstd2[:, c0:c0 + n]
                kMN, kRS = ("rstd", j), ("rstd2", j)
                S.add("act", ACT(MN, ps[4 + jj][:, :], AF.Copy, scale=1.0 / D), reads=[("ps", 4 + jj)], writes=[kMN])
                S.add("dve", TT(RS, MN, MN, ALU.mult), reads=[kMN], writes=[kRS])
                S.add("dve", STT(RS, ps[6 + jj][:, :], 1.0 / D, RS, ALU.mult, ALU.subtract), reads=[("ps", 6 + jj), kRS], writes=[kRS])
                S.add("act", ACT(RS, RS, AF.Sqrt, bias=EPSAP, scale=1.0), reads=[kRS, "cst"], writes=[kRS])
                S.add("dve", RCP(RS, RS), reads=[kRS], writes=[kRS])
                for c in range(8):
                    Vv, kV = vf[:, c * 2 + jj, :], ("vf", c * 2 + jj)
                    U, kU = blk(c * 2 + jj)
                    khb = ("hb", c, j)
                    S.add("dve", TT(Vv, Vv, MN, ALU.subtract), reads=[kV, kMN], writes=[kV])
                    S.add("dve", TT(Vv, Vv, RS, ALU.mult), reads=[kV, kRS], writes=[kV])
                    S.add("act", ACT(hb[:, c, c0:c0 + n], Vv, AF.Identity, bias=Vc("cnb", c), scale=Vc("cng", c)),
                          reads=[kV, "vecs"], writes=[khb])
                    ti = nextps()
                    pT = ps[ti][:, :].bitcast(BF16)
                    S.add("pe", [TR(pT[:, b * 128:(b + 1) * 128], hb[:, c, c0 + b * 128:c0 + (b + 1) * 128], ident_b[:]) for b in range(4)],
                          reads=[khb, "ident_b"], writes=[("ps", ti)])
                    vi_ = rotate("vTb", vTb)
                    S.add("act", ACT(vTb[vi_][:], pT[:, 0:512], AF.Copy), reads=[("ps", ti)], writes=[("vTb", vi_)])
                    mi_ = nextps()
                    S.add("pe", [MM(ps[mi_][:, b * 128:(b + 1) * 128], vTb[vi_][:, b * 128:(b + 1) * 128], wsT_b[:, c, :]) for b in range(4)],
                          reads=[("vTb", vi_), "wsT_b"], writes=[("ps", mi_)])
                    tf = rotate("tmpf", tmpf)
                    for b in range(4):
                        S.add("dve", TT(tmpf[tf][:, b * 128:(b + 1) * 128], ps[mi_][:, b * 128:(b + 1) * 128], bsb[:, c, :], ALU.add),
                              reads=[("ps", mi_), "bsb"], writes=[("tmpf", tf)])
                    S.add("dve", TT(hb[:, c, c0:c0 + n], tmpf[tf][:], U, ALU.mult), reads=[("tmpf", tf), kU], writes=[khb])
            S.alias(vkeys, actkeys)
            psrot["pool"] = [0, 1, 2, 3, 4]
            out_proj(cout_d, 1, subtiles, post=post)

        main_st = [(HALO, 512), (HALO + 512, 512)]
        if redun:
            xkeys_main = [("xres", c, j) for c in range(8) for j in (1, 2)]
            xkeys_all = [("xres", c, j) for c in range(8) for j in range(3)]
            for pt in range(3 * nt):
                S.dma("sp", DMA(xres[:, :, HALO:WCOL], xpre_d[:, :, pt * TILE:(pt + 1) * TILE]), writes=xkeys_main, semkey="xload")
                ffn(0, 0, main_st, post=False)
                trans(0, 0, 0, 1, main_st)
                mixer_ab_in(pt, main_st, mode="prefix", pre=False)
            for t in range(nt):
                subt = ([(0, HALO)] if t == 0 else []) + main_st
                lo = 0 if t == 0 else HALO
                src0 = t * TILE + lo
                S.dma("sp", DMA(xres[:, :, lo:WCOL], xin_d[:, :, src0:(t + 1) * TILE + HALO]), writes=xkeys_all, semkey="xload")
                if nsub <= 1:
                    ffn(0, 0, subt)
                else:
                    ffn(0, 0, subt, post=False)
                    trans(0, 0, 0, 1, subt)
                    mixer_ab_in(t, subt, mode="direct", pre=False)
                    out_proj(about_d, 0, main_st, src=act, srckey="act", post=(nsub == 2))
                if nsub > 2:
                    trans(0, 1, 0, 2, main_st)
                    ffn(0, 1, main_st, pre=False, post=(nsub == 3))
                if nsub > 3:
                    trans(0, 2, 1, 0, main_st)
                    ffn(1, 0, main_st, pre=False, post=(nsub == 4))
                if nsub > 4:
                    trans(1, 0, 1, 1, main_st)
                    mixer_c(main_st, pre=False, post=(nsub == 5))
                if nsub > 5:
                    trans(1, 1, 1, 2, main_st)
                    ffn(1, 1, main_st, pre=False, post=True)
                S.dma("sp", DMA(out_d[:, :, t * TILE:(t + 1) * TILE], xres[:, :, HALO:WCOL]), reads=xkeys_main, writes=[("dO", t)], rr=("st", 8))
            S.final_wait("sp", [("dO", t) for t in range(nt)])
        if doA and not redun:
            for t in range(nt):
                subt = ([(0, HALO)] if t == 0 else []) + main_st
                lo = 0 if t == 0 else HALO
                src0 = t * TILE + lo
                keys = [("xres", c, j) for c in range(8) for j in range(3)]
                S.dma("sp", DMA(xres[:, :, lo:WCOL], xin_d[:, :, src0:(t + 1) * TILE + HALO]), writes=keys, semkey="xload")
                ffn(0, 0, subt)
                S.dma("sp", DMA(x1_d[:, :, t * TILE:(t + 1) * TILE], xres[:, :, HALO:WCOL]),
                      reads=[("xres", c, j) for c in range(8) for j in (1, 2)], writes=[("dX", t)], rr=("st", 8))
                mixer_ab_in(t, subt)
            S.dma("sp", DMA(st_d[:, 0:4], acar[:]), reads=["acar"], writes=["dst"], rr=("st", 8))
            S.dma("sp", DMA(st_d[:, 4:8], hcar[:]), reads=["hcar"], writes=["dst2"], rr=("st", 8))
            if not fused:
                S.final_wait("sp", ["dst", "dst2"] + [("dX", t) for t in range(nt)] +
                             [(nm, c, t) for nm in ("dP", "dQ", "dY") for c in range(4) for t in range(nt)])
        if fused:
            S.add("pool", lambda e: e.collective_compute("AllGather", ALU.bypass, replica_groups=[list(range(NCORES))],
                                                        ins=[st_d[:, :]], outs=[stall_d[:, :]]),
                  reads=["dst", "dst2"], writes=["dstall"], semkey="cc", inc=16)
        if doB and not redun:
            S.dma("sp", DMA(stall[:], stall_d.rearrange("(r p) c -> p r c", p=128)), reads=["dstall"], writes=["stall"], rr=("ld", 8))
            S.add("dve", MSET(carry[:], 0.0), writes=["carry"])
            for s in range(NCORES):
                S.add("dve", TT(ctmp[0][:], stall[:, s, 0:4], carry[:], ALU.mult), reads=["stall", "carry"], writes=["ctmp0"])
                S.add("dve", TT(ctmp[0][:], ctmp[0][:], stall[:, s, 4:8], ALU.add), reads=["stall", "ctmp0"], writes=["ctmp0"])
                S.add("dve", TT(ctmp[0][:], ctmp[0][:], carry[:], ALU.subtract), reads=["carry", "ctmp0"], writes=["ctmp0"])
                S.add("dve", STT(carry[:], ctmp[0][:], Vc("mprev", s), carry[:], ALU.mult, ALU.add),
                      reads=["ctmp0", "carry", "vecs"], writes=["carry"])
            for t in range(nt):
                keys = [("xres", c, j) for c in range(8) for j in (1, 2)]
                S.dma("sp", DMA(xres[:, :, HALO:WCOL], x1_d[:, :, t * TILE:(t + 1) * TILE]), reads=[("dX", t)], writes=keys, semkey="xload")
                if nsub > 1:
                    for c in range(4):
                        S.dma("sp", DMA(ybuf[:, c * 2:c * 2 + 2, :], P_d[:, c, t * TILE:(t + 1) * TILE].rearrange("p (a b) -> p a b", a=2)),
                              reads=[("dP", c, t)], writes=[("yb", c * 2), ("yb", c * 2 + 1)], rr=("ld", 8))
                        S.dma("sp", DMA(ybuf[:, 8 + c * 2:8 + c * 2 + 2, :], Q_d[:, c, t * TILE:(t + 1) * TILE].rearrange("p (a b) -> p a b", a=2)),
                              reads=[("dQ", c, t)], writes=[("yb", 8 + c * 2), ("yb", 8 + c * 2 + 1)], rr=("ld", 8))
                        S.dma("pool", DMA(hb[:, 4 + c, HALO:WCOL], yb_d[:, c, t * TILE:(t + 1) * TILE]),
                              reads=[("dY", c, t)], writes=[("hb", 4 + c, 1), ("hb", 4 + c, 2)], rr=("lq", 4))
                        for jj in range(2):
                            S.add("dve", STT(hb[:, c, HALO + jj * 512:HALO + (jj + 1) * 512], ybuf[:, 8 + c * 2 + jj, :], carry[:, c:c + 1],
                                             ybuf[:, c * 2 + jj, :], ALU.mult, ALU.add),
                                  reads=[("yb", c * 2 + jj), ("yb", 8 + c * 2 + jj), "carry"], writes=[("hb", c, jj + 1)])
                    out_proj(about_d, 0, main_st)
                if nsub > 2:
                    ffn(0, 1, main_st)
                if nsub > 3:
                    ffn(1, 0, main_st)
                if nsub > 4:
                    mixer_c(main_st)
                if nsub > 5:
                    ffn(1, 1, main_st)
                S.dma("sp", DMA(out_d[:, :, t * TILE:(t + 1) * TILE], xres[:, :, HALO:WCOL]), reads=keys, writes=[("dO", t)], rr=("st", 8))
            S.final_wait("sp", [("dO", t) for t in range(nt)])

        global _LAST_SCHED
        _LAST_SCHED = S
        if os.environ.get("KERNEL_PLAN_ONLY"):
            return None
        with nc.Block() as block:
            def run_stream(eng, items):
                for waits, fns, incinfo in items:
                    for sk, v in waits:
                        eng.wait_ge(sems[sk], v)
                    ins = None
                    for fn in fns:
                        ins = fn(eng)
                    if incinfo is not None:
                        ins.then_inc(sems[incinfo[0]], incinfo[1])

            @block.tensor
            def _(e):
                run_stream(e, S.streams["pe"])

            @block.scalar
            def _(e):
                run_stream(e, S.streams["act"])

            @block.vector
            def _(e):
                run_stream(e, S.streams["dve"])

            @block.gpsimd
            def _(e):
                run_stream(e, S.streams["pool"])

            @block.sync
            def _(e):
                run_stream(e, S.streams["sp"])
    return nc


def _fm(v, n):
    return np.ascontiguousarray(np.asarray(v, np.float32).reshape(n, 128).T)


def _slotK(w, cols_list):
    K = w.shape[0]
    sel = np.concatenate([w[:, a:b] for a, b in cols_list], axis=1)
    return np.ascontiguousarray(sel.reshape(K // 128, 128, -1).transpose(1, 0, 2).reshape(128, -1))


def prep_shared(inp):
    f = lambda a: np.asarray(a, np.float32)
    sh = {}
    ada_w = f(inp["ada_w"])
    sh["adaw"] = np.ascontiguousarray(
        ada_w.reshape(2, 8, 128, 18, 512).transpose(0, 3, 2, 1, 4).reshape(36, 128, 4096))
    sh["adab"] = np.ascontiguousarray(f(inp["ada_b"]).reshape(36, 512))
    w13 = f(inp["ffn_w13"])
    w2 = f(inp["ffn_w2"])
    w13L = np.empty((88, 128, 2048), np.float32)
    w2L = np.empty((32, 128, 2816), np.float32)
    for l in range(2):
        for ff in range(2):
            for fc in range(FC):
                w13L[(l * 2 + ff) * 22 + fc] = _slotK(w13[l, ff], [(fc * 128, fc * 128 + 128), (DFF + fc * 128, DFF + fc * 128 + 128)])
            for dc in range(8):
                w2L[(l * 2 + ff) * 8 + dc] = _slotK(w2[l, ff], [(dc * 128, dc * 128 + 128)])
    sh["w13L"] = w13L
    sh["w2L"] = w2L
    wi = f(inp["ab_w_in"])[0]
    sh["abinL"] = np.stack(
        [_slotK(wi, [(512 + c * 128, 512 + c * 128 + 128), (c * 128, c * 128 + 128)]) for c in range(4)] +
        [_slotK(wi, [(1024 + c * 128, 1024 + c * 128 + 128), (1536 + c * 128, 1536 + c * 128 + 128)]) for c in range(4)])
    wo = f(inp["ab_w_out"])[0]
    sh["aboutL"] = np.stack([_slotK(wo, [(dc * 128, dc * 128 + 128)]) for dc in range(8)])
    ci = f(inp["c_w_in"])[0]
    sh["cinL"] = np.stack([_slotK(ci, [(c * 128, c * 128 + 128), (1024 + c * 128, 1024 + c * 128 + 128)]) for c in range(8)])
    co = f(inp["c_w_out"])[0]
    sh["coutL"] = np.stack([_slotK(co, [(dc * 128, dc * 128 + 128)]) for dc in range(8)])
    gw = f(inp["a_gate_w"])[0]
    gbd = np.zeros((128, 8, 128), np.float32)
    for c in range(4):
        for h2 in range(2):
            gbd[h2 * 64:(h2 + 1) * 64, c * 2 + 0, h2 * 64:(h2 + 1) * 64] = gw[2 * c + h2][:, 0:64]
            gbd[h2 * 64:(h2 + 1) * 64, c * 2 + 1, h2 * 64:(h2 + 1) * 64] = gw[2 * c + h2][:, 64:128]
    sh["gatebd"] = gbd.reshape(128, 1024)
    ws = f(inp["c_w_s"])[0]
    sh["wsT"] = np.ascontiguousarray(ws.transpose(2, 0, 1).reshape(128, 1024))
    sh["bs"] = np.ascontiguousarray(f(inp["c_b_s"])[0].reshape(1, 1024))
    cst = np.zeros((128, 384), np.float32)
    cst[:, 0:128] = np.eye(128, dtype=np.float32)
    cst[:, 128:256] = np.triu(np.ones((128, 128), np.float32))
    cst[0:64, 256:320] = 1.0
    cst[64:128, 320:384] = 1.0
    sh["consts"] = cst
    return sh


def prep_vecs(inp, core, ntok=4096):
    f = lambda a: np.asarray(a, np.float32)
    b = core // 4
    pos = core % 4
    v = np.zeros((128, NV), np.float32)

    def put(name, arr):
        v[:, VOFF[name]:VOFF[name] + arr.shape[1]] = arr

    put("c", _fm(f(inp["c"])[b], 8))
    for l in range(2):
        for s in range(3):
            put(("npre", l, s), _fm(f(inp["norm_pre"])[l, s], 8))
            put(("npost", l, s), _fm(f(inp["norm_post"])[l, s], 8))
    for k in range(4):
        put(("acw", k), _fm(f(inp["a_conv_w"])[0, k], 4))
    put("acb", _fm(f(inp["a_conv_b"])[0], 4))
    gb = f(inp["a_gate_b"])[0]
    put("gbr", _fm(gb[:, 0:64].reshape(512), 4))
    put("gbi", _fm(gb[:, 64:128].reshape(512), 4))
    put("lam", _fm(f(inp["a_lam"])[0], 4))
    for k in range(31):
        put(("bcw", k), _fm(f(inp["b_conv_w"])[0, k], 4))
    put("bcb", _fm(f(inp["b_conv_b"])[0], 4))
    put("bng", _fm(f(inp["b_norm_g"])[0], 4))
    put("bnb", _fm(f(inp["b_norm_b"])[0], 4))
    cb = f(inp["c_b_in"])[0]
    put("cbu", _fm(cb[0:1024], 8))
    put("cbv", _fm(cb[1024:2048], 8))
    put("cng", _fm(f(inp["c_norm_g"])[0], 8))
    put("cnb", _fm(f(inp["c_norm_b"])[0], 8))
    v[:, VOFF["mhalo"]] = 1.0 if pos > 0 else 0.0
    for s in range(NCORES):
        v[:, VOFF["mprev"] + s] = 1.0 if (s // 4 == b and s % 4 < pos) else 0.0
    nt = ntok // TILE
    for pt in range(min(3 * nt, 12)):
        v[:, VOFF["mpre"] + pt] = 1.0 if pt >= (3 - pos) * nt else 0.0
    return v


def prep_x(x, core, ntok):
    b = core // 4
    pos = core % 4
    t0 = pos * ntok
    xs = np.zeros((ntok + HALO, D), np.float32)
    xs[HALO:] = x[b, t0:t0 + ntok]
    if pos > 0:
        xs[:HALO] = x[b, t0 - HALO:t0]
    return np.ascontiguousarray(xs.T.reshape(8, 128, ntok + HALO).transpose(1, 0, 2))


def prep_xpre(x, core, ntok):
    b = core // 4
    pos = core % 4
    t0 = pos * ntok
    xs = np.zeros((3 * ntok, D), np.float32)
    if pos > 0:
        xs[3 * ntok - t0:] = x[b, 0:t0]
    return np.ascontiguousarray(xs.T.reshape(8, 128, 3 * ntok).transpose(1, 0, 2))


_NC_CACHE = {}


def _get_nc(ntok, phase, nsub=6):
    key = (ntok, phase, nsub)
    if key not in _NC_CACHE:
        _NC_CACHE[key] = build(ntok, phase, nsub)
    return _NC_CACHE[key]


def run_all(inputs, nsub=6, fused=False):
    x = np.asarray(inputs["x"], np.float32)
    B, SEQ, _ = x.shape
    ntok = SEQ // 4
    sh = prep_shared(inputs)
    vec = [prep_vecs(inputs, k, ntok) for k in range(NCORES)]
    cores = list(range(NCORES))
    keysA = ["consts", "adaw", "adab", "w13L", "w2L", "abinL", "gatebd"]
    keysB = ["consts", "adaw", "adab", "w13L", "w2L", "aboutL", "cinL", "coutL", "wsT", "bs"]
    if fused:
        nc = _get_nc(ntok, "F", nsub)
        maps = []
        for k in cores:
            m = {kk: sh[kk] for kk in set(keysA + keysB)}
            m["vecs"] = vec[k]
            m["xin"] = prep_x(x, k, ntok)
            m["xpre"] = prep_xpre(x, k, ntok)
            maps.append(m)
        res = run_bass_kernel_spmd(nc, maps, core_ids=cores)
        outs = [r["out"] for r in res.results]
    else:
        ncA = _get_nc(ntok, "A")
        maps = []
        for k in cores:
            m = {kk: sh[kk] for kk in keysA}
            m["vecs"] = vec[k]
            m["xin"] = prep_x(x, k, ntok)
            maps.append(m)
        resA = run_bass_kernel_spmd(ncA, maps, core_ids=cores)
        stall = np.ascontiguousarray(np.concatenate([r["st"] for r in resA.results], axis=0))
        ncB = _get_nc(ntok, "B", nsub)
        maps = []
        for k in cores:
            m = {kk: sh[kk] for kk in keysB}
            m["vecs"] = vec[k]
            r = resA.results[k]
            for kk in ("x1T", "PT", "QT", "ybT"):
                m[kk] = r[kk]
            m["stall"] = stall
            maps.append(m)
        resB = run_bass_kernel_spmd(ncB, maps, core_ids=cores)
        outs = [r["out"] for r in resB.results]
    out = np.empty((B, SEQ, D), np.float32)
    for k in cores:
        b, pos = k // 4, k % 4
        o = outs[k]
        out[b, pos * ntok:(pos + 1) * ntok] = o.transpose(2, 1, 0).reshape(ntok, D)
    return out


def kernel(**inputs):
    return run_all(inputs, nsub=6, fused=False)
```
